# Optimizing a Trainium2 kernel written in Bass

```python
import jax, jax.numpy as jnp
from jax import lax
import numpy as np

D_MODEL = 2048
BATCH = 16
SEQ = 2048
DEPTH = 4

W_CONV = D_MODEL // 2
HEAD_DIM = 64
N_Q_HEADS = (D_MODEL // 2) // HEAD_DIM
N_KV_HEADS = N_Q_HEADS // 4
GQA_GROUP = N_Q_HEADS // N_KV_HEADS
W_ATT = N_Q_HEADS * HEAD_DIM
W_KV = N_KV_HEADS * HEAD_DIM
W_MIX = W_CONV + W_ATT
IN_SPLITS = [W_CONV, W_CONV, W_CONV, W_ATT, W_KV, W_KV, W_ATT]
W_IN = sum(IN_SPLITS)
CONV_WIDTH = 31
CONV_PAD = CONV_WIDTH // 2
WINDOW = 128
BLK = 128
NUM_BUCKETS = 32
MAX_DISTANCE = 128
PLE_DIM = 256
EPS = 1e-6
NEG = -1e30

kernel_name = "hybrid_conv_swa_parallel_encoder"


def _rmsnorm(x, g):
    xf = x.astype(jnp.float32)
    y = xf * lax.rsqrt(jnp.mean(xf * xf, axis=-1, keepdims=True) + EPS)
    return (y * g.astype(jnp.float32)).astype(x.dtype)


def _layernorm(x, g, b):
    xf = x.astype(jnp.float32)
    mu = jnp.mean(xf, axis=-1, keepdims=True)
    xc = xf - mu
    y = xc * lax.rsqrt(jnp.mean(xc * xc, axis=-1, keepdims=True) + EPS)
    return (y * g.astype(jnp.float32) + b.astype(jnp.float32)).astype(x.dtype)


def _t5_band_buckets():
    q_off = np.arange(BLK)[:, None]
    k_off = np.arange(3 * BLK)[None, :] - BLK
    rel = k_off - q_off
    half = NUM_BUCKETS // 2
    ret = (rel > 0).astype(np.int32) * half
    n = np.abs(rel)
    max_exact = half // 2
    large = max_exact + (np.log(np.maximum(n, 1) / max_exact)
                         / np.log(MAX_DISTANCE / max_exact)
                         * (half - max_exact)).astype(np.int32)
    large = np.minimum(large, half - 1)
    ret = ret + np.where(n < max_exact, n, large)
    return ret.astype(np.int32), (n <= WINDOW)


def _band_bias(rel_bias):
    buckets, band = _t5_band_buckets()
    bias = jnp.take(rel_bias.astype(jnp.float32), jnp.asarray(buckets), axis=0)
    bias = jnp.transpose(bias, (2, 0, 1))
    bias = jnp.where(jnp.asarray(band)[None], bias, NEG)
    return bias.reshape(N_KV_HEADS, GQA_GROUP, BLK, 3 * BLK)


def _window_attention(q, k, v, bias_band, sink):
    B, S = q.shape[0], q.shape[1]
    nb = S // BLK
    scale = HEAD_DIM ** -0.5
    k_pad = jnp.pad(k, ((0, 0), (BLK, BLK), (0, 0), (0, 0)))
    v_pad = jnp.pad(v, ((0, 0), (BLK, BLK), (0, 0), (0, 0)))
    sink_f = sink.astype(jnp.float32).reshape(N_KV_HEADS, GQA_GROUP, 1)
    key_off = jnp.arange(3 * BLK) - BLK

    def one_block(b):
        start = b * BLK
        qb = lax.dynamic_slice_in_dim(q, start, BLK, axis=1)
        kb = lax.dynamic_slice_in_dim(k_pad, start, 3 * BLK, axis=1)
        vb = lax.dynamic_slice_in_dim(v_pad, start, 3 * BLK, axis=1)
        s = jnp.einsum('bqkgd,bskd->bkgqs', qb, kb,
                       preferred_element_type=jnp.float32) * scale + bias_band
        kpos = start + key_off
        valid = (kpos >= 0) & (kpos < S)
        s = jnp.where(valid, s, NEG)
        m = jnp.maximum(jnp.max(s, axis=-1), sink_f)
        e = jnp.exp(s - m[..., None])
        denom = jnp.sum(e, axis=-1) + jnp.exp(sink_f - m)
        pr = (e / denom[..., None]).astype(vb.dtype)
        return jnp.einsum('bkgqs,bskd->bqkgd', pr, vb)

    out = lax.map(one_block, jnp.arange(nb))
    return jnp.moveaxis(out, 0, 1).reshape(B, S, W_ATT)


def _conformer_conv(a_val, a_glu, conv_w, conv_b, cln_g, cln_b):
    a = a_val * jax.nn.sigmoid(a_glu)
    y = lax.conv_general_dilated(
        a, conv_w[:, None, :].astype(a.dtype), window_strides=(1,),
        padding=[(CONV_PAD, CONV_PAD)], dimension_numbers=('NWC', 'WIO', 'NWC'),
        feature_group_count=W_CONV) + conv_b
    y = _layernorm(y, cln_g, cln_b)
    return jax.nn.silu(y)


def setup_inputs(seed: int = 0) -> dict:
    key = jax.random.key(seed)
    ks = jax.random.split(key, 16)
    f32 = jnp.float32
    nrm = lambda k, shp, s: (jax.random.normal(k, shp, f32) * s)
    return {
        "x": nrm(ks[0], (BATCH, SEQ, D_MODEL), 1.0),
        "p": nrm(ks[1], (DEPTH, BATCH, SEQ, PLE_DIM), 1.0),
        "norm_g": 1.0 + nrm(ks[2], (DEPTH, D_MODEL), 0.02),
        "w_in": nrm(ks[3], (DEPTH, D_MODEL, W_IN), D_MODEL ** -0.5),
        "conv_w": nrm(ks[4], (DEPTH, CONV_WIDTH, W_CONV), CONV_WIDTH ** -0.5),
        "conv_b": nrm(ks[5], (DEPTH, W_CONV), 0.02),
        "cln_g": 1.0 + nrm(ks[6], (DEPTH, W_CONV), 0.02),
        "cln_b": nrm(ks[7], (DEPTH, W_CONV), 0.02),
        "sink": nrm(ks[8], (DEPTH, N_Q_HEADS), 0.5),
        "rel_bias": nrm(ks[9], (NUM_BUCKETS, N_Q_HEADS), 0.5),
        "w_out": nrm(ks[10], (DEPTH, W_MIX, D_MODEL), W_MIX ** -0.5),
        "w_pe": nrm(ks[11], (DEPTH, PLE_DIM, D_MODEL), PLE_DIM ** -0.5),
        "pe_g": 1.0 + nrm(ks[12], (DEPTH, D_MODEL), 0.02),
        "w_pg": nrm(ks[13], (DEPTH, D_MODEL, D_MODEL), D_MODEL ** -0.5),
        "final_g": 1.0 + nrm(ks[14], (D_MODEL,), 0.02),
    }


def reference(x, p, norm_g, w_in, conv_w, conv_b, cln_g, cln_b, sink, rel_bias,
              w_out, w_pe, pe_g, w_pg, final_g):
    B, S, _ = x.shape
    bias_band = _band_bias(rel_bias)
    offs = np.cumsum(IN_SPLITS)[:-1].tolist()
    h = x
    for i in range(DEPTH):
        hn = _rmsnorm(h, norm_g[i])
        u = hn @ w_in[i]
        a_val, a_glu, a_z, q, k, v, b_z = jnp.split(u, offs, axis=-1)
        ya = _conformer_conv(a_val, a_glu, conv_w[i], conv_b[i], cln_g[i], cln_b[i])
        ya = ya * jax.nn.silu(a_z)
        q = q.reshape(B, S, N_KV_HEADS, GQA_GROUP, HEAD_DIM)
        k = k.reshape(B, S, N_KV_HEADS, HEAD_DIM)
        v = v.reshape(B, S, N_KV_HEADS, HEAD_DIM)
        yb = _window_attention(q, k, v, bias_band, sink[i]) * jax.nn.silu(b_z)
        h = h + jnp.concatenate([ya, yb], axis=-1) @ w_out[i]
        e = _rmsnorm(p[i] @ w_pe[i], pe_g[i])
        h = h + e * jax.nn.sigmoid(h @ w_pg[i])
    return _rmsnorm(h, final_g)
```

```python
import os
import numpy as np
import concourse.bass as bass
import concourse.mybir as mybir
from concourse.bass_utils import run_bass_kernel_spmd

F32 = mybir.dt.float32
BF16 = mybir.dt.bfloat16
AF = mybir.ActivationFunctionType
ALU = mybir.AluOpType

D = 2048
S = 2048
NL = 4
G = 512
NG = S // G
KC = 16
EPS = 1e-6
NCORES = 8
SEQ_PER_CORE = 2
NWS = int(os.environ.get("KNWS", 3))

CH = []
for _j in range(4):
    CH.append(("k", _j))
for _vc in range(2):
    CH.append(("v", _vc))
for _c in range(8):
    CH.append(("ag", _c))
    CH.append(("av", _c))
for _c in range(8):
    CH.append(("az", _c))
for _c in range(8):
    CH.append(("q", _c))
for _c in range(8):
    CH.append(("bz", _c))
NCH = len(CH)


def _chunk_cols():
    AV0, AG0, AZ0, Q0, K0, V0, BZ0 = 0, 1024, 2048, 3072, 4096, 4352, 4608
    idx = np.zeros((NCH, 128), np.int64)
    r = np.arange(128)
    for n, (t, i) in enumerate(CH):
        if t == "k":
            idx[n] = K0 + i * 64 + (r % 64)
        elif t == "v":
            idx[n] = V0 + i * 128 + r
        elif t == "ag":
            idx[n] = AG0 + i * 128 + r
        elif t == "av":
            idx[n] = AV0 + i * 128 + r
        elif t == "az":
            idx[n] = AZ0 + i * 128 + r
        elif t == "q":
            idx[n] = Q0 + i * 128 + r
        elif t == "bz":
            idx[n] = BZ0 + i * 128 + r
    return idx


VOFF = {}
_o = 0
for _name, _n in (("norm_g", NL * 16), ("pe_g", NL * 16), ("final_g", 16), ("conv_b", NL * 8),
                  ("cln_g", NL * 8), ("cln_b", NL * 8), ("conv_w", NL * 8 * 31), ("sink", NL * 8)):
    VOFF[_name] = _o
    _o += _n
NV = _o


def _pack_vecs(norm_g, pe_g, final_g, conv_b, cln_g, cln_b, conv_w, sink):
    v = np.zeros((128, NV), np.float32)
    def put(name, arr):
        v[:, VOFF[name]:VOFF[name] + arr.shape[1]] = arr
    put("norm_g", norm_g.reshape(NL, 16, 128).transpose(2, 0, 1).reshape(128, -1))
    put("pe_g", pe_g.reshape(NL, 16, 128).transpose(2, 0, 1).reshape(128, -1))
    put("final_g", final_g.reshape(16, 128).T)
    put("conv_b", conv_b.reshape(NL, 8, 128).transpose(2, 0, 1).reshape(128, -1))
    put("cln_g", cln_g.reshape(NL, 8, 128).transpose(2, 0, 1).reshape(128, -1))
    put("cln_b", cln_b.reshape(NL, 8, 128).transpose(2, 0, 1).reshape(128, -1))
    put("conv_w", conv_w.reshape(NL, 31, 8, 128).transpose(3, 0, 2, 1).reshape(128, -1))
    sk = np.zeros((128, NL, 4, 2), np.float32)
    for j in range(4):
        for cc in range(2):
            sk[:64, :, j, cc] = sink[:, 4 * j + 2 * cc][None, :]
            sk[64:, :, j, cc] = sink[:, 4 * j + 2 * cc + 1][None, :]
    put("sink", sk.reshape(128, -1))
    return v


def _t5_band_buckets():
    BLK, NUM_BUCKETS, MAX_DISTANCE, WINDOW = 128, 32, 128, 128
    q_off = np.arange(BLK)[:, None]
    k_off = np.arange(3 * BLK)[None, :] - BLK
    rel = k_off - q_off
    half = NUM_BUCKETS // 2
    ret = (rel > 0).astype(np.int32) * half
    n = np.abs(rel)
    max_exact = half // 2
    large = max_exact + (np.log(np.maximum(n, 1) / max_exact)
                         / np.log(MAX_DISTANCE / max_exact) * (half - max_exact)).astype(np.int32)
    large = np.minimum(large, half - 1)
    ret = ret + np.where(n < max_exact, n, large)
    return ret.astype(np.int32), (n <= WINDOW)


class Tracker:
    ENG = ("pe", "act", "dve", "pool", "sp")

    def __init__(self):
        self.ops = {e: [] for e in self.ENG}
        self.cnt = {e: 0 for e in self.ENG}
        self.dcnt = {}
        self.last_w = {}
        self.rd = {}
        self.seen = {e: {} for e in self.ENG}
        self.pending = {e: {} for e in self.ENG}

    def _deps(self, eng, reads, writes):
        deps = dict(self.pending[eng])
        self.pending[eng] = {}

        def add(ev):
            if ev is None:
                return
            s, v = ev
            if deps.get(s, 0) < v:
                deps[s] = v
        for k in reads:
            add(self.last_w.get(k))
        for k in writes:
            add(self.last_w.get(k))
            for ev in self.rd.get(k, ()):
                add(ev)
        out = []
        seen = self.seen[eng]
        for s, v in deps.items():
            if s == "pe" and eng == "pe":
                continue
            if seen.get(s, 0) >= v:
                continue
            seen[s] = v
            out.append((s, v))
        return out

    def _record(self, ev, reads, writes):
        for k in writes:
            self.last_w[k] = ev
            self.rd[k] = []
        for k in reads:
            self.rd.setdefault(k, []).append(ev)

    def op(self, eng, fn, reads=(), writes=(), inc=True):
        waits = self._deps(eng, reads, writes)
        if inc:
            self.cnt[eng] += 1
            ev = (eng, self.cnt[eng])
        else:
            ev = (eng, self.cnt[eng] + 1)
        self._record(ev, reads, writes)
        self.ops[eng].append((waits, fn, (eng, 1) if inc else None))

    def dma(self, eng, sem, fn, reads=(), writes=()):
        waits = self._deps(eng, reads, writes)
        self.dcnt[sem] = self.dcnt.get(sem, 0) + 16
        ev = (sem, self.dcnt[sem])
        self._record(ev, reads, writes)
        self.ops[eng].append((waits, fn, (sem, 16)))

    def barrier(self):
        allv = dict(self.cnt)
        allv.update(self.dcnt)
        for e in self.ENG:
            for s, v in allv.items():
                if v > 0 and self.pending[e].get(s, 0) < v:
                    self.pending[e][s] = v

    def sem_names(self):
        return list(self.ENG) + sorted(self.dcnt.keys())


def build_program(nseq=SEQ_PER_CORE, layers=(0, 1, 2, 3), do_stage0=True, do_final=True, debug_h=False):
    nc = bass.Bass("TRN2", target_bir_lowering=False)
    T = Tracker()

    x_d = nc.dram_tensor("x", [nseq, S, D], F32, kind="ExternalInput").ap()
    p_d = nc.dram_tensor("p", [NL, nseq, S, 256], F32, kind="ExternalInput").ap()
    win_d = nc.dram_tensor("w_in", [NL, NCH, 128, KC + 1, 128], F32, kind="ExternalInput").ap()
    wout_d = nc.dram_tensor("w_out", [NL, 16, 128, KC + 1, 128], F32, kind="ExternalInput").ap()
    wpg_d = nc.dram_tensor("w_pg", [NL, 16, 128, KC + 1, 128], F32, kind="ExternalInput").ap()
    wpe_d = nc.dram_tensor("w_pe", [NL, 128, 2, D], F32, kind="ExternalInput").ap()
    vecs_d = nc.dram_tensor("vecs", [128, NV], F32, kind="ExternalInput").ap()
    bias_d = nc.dram_tensor("biasg", [16, 128, 384], F32, kind="ExternalInput").ap()
    mask_d = nc.dram_tensor("masks", [2, 128, 384], F32, kind="ExternalInput").ap()
    ident_d = nc.dram_tensor("ident", [128, 128], F32, kind="ExternalInput").ap()
    out_d = nc.dram_tensor("out", [nseq, S, D], F32, kind="ExternalOutput").ap()
    if debug_h:
        hT_t = nc.dram_tensor("hT", [nseq, D, S], F32, kind="ExternalOutput")
    else:
        hT_t = nc.dram_tensor("hT", [nseq, D, S], F32)
    hT_d = hT_t.ap()
    wbin_d = nc.dram_tensor("wb_in", [NL, NCH, 128, KC * 128], BF16).ap()
    wbout_d = nc.dram_tensor("wb_out", [NL, 16, 128, KC * 128], BF16).ap()
    wbpg_d = nc.dram_tensor("wb_pg", [NL, 16, 128, KC * 128], BF16).ap()

    from contextlib import ExitStack
    es = ExitStack()

    def sb(name, shape, dt):
        return es.enter_context(nc.sbuf_tensor(name, shape, dt))

    def ps(name):
        return es.enter_context(nc.psum_tensor(name, [128, 512], F32))

    with es:
        identf = sb("identf", [128, 128], F32)
        identb = sb("identb", [128, 128], BF16)
        onesb = sb("onesb", [128, 128], BF16)
        vecs = sb("vecs_sb", [128, NV], F32)
        esink = sb("esink", [128, NL * 8], F32)
        epsc = sb("epsc", [128, 1], F32)
        biasT = sb("biasT", [128, 16, 384], BF16)
        NHS = int(os.environ.get("KNHS", 7))
        hs = [sb(f"hs{i}", [128, 640], F32) for i in range(NHS)]
        hs_i = [0]

        def next_hs():
            i = hs_i[0] % NHS
            hs_i[0] += 1
            return hs[i], ("hs", i), f"hs{i}"
        sqb = [sb(f"sqb{i}", [128, 640], BF16) for i in range(2)]
        hb = sb("hb", [128, KC, 640], BF16)
        rstd = sb("rstd", [128, 640], F32)
        rsN = sb("rsN", [128, 640], F32)
        tmp640 = sb("tmp640", [128, 640], F32)
        wsl = [sb(f"w{i}", [128, KC, 128], BF16) for i in range(NWS)]
        wpe = sb("wpe", [128, 2, D], BF16)
        abuf = sb("abuf", [128, 8, 656], BF16)
        kbuf = sb("kbuf", [128, 4, 6, 128], BF16)
        vbuf = sb("vbuf", [128, 6, 256], BF16)
        qbuf = sb("qbuf", [128, 8, 512], BF16)
        ybuf = sb("ybuf", [128, 16, 512], BF16)
        dg = [sb(f"dg{i}", [128, 31, 128], BF16) for i in range(2)]
        scrA = sb("scrA", [128, 6144], F32)
        ysq = [sb(f"ysq{i}", [128, 512], BF16) for i in range(2)]
        meant = sb("meant", [128, 512], F32)
        lnA = sb("lnA", [128, 512], F32)
        lnB = sb("lnB", [128, 512], F32)
        tf = [sb(f"tf{i}", [128, 512], F32) for i in range(2)]
        tb16 = [sb(f"tb16_{i}", [128, 512], BF16) for i in range(2)]
        PT = [sb(f"pt{i}", [128, 512], BF16) for i in range(6)]
        dens = [sb(f"dens{i}", [128, 256], F32) for i in range(2)]
        rz = [sb(f"rz{i}", [128, 256], F32) for i in range(2)]
        ptile = sb("ptile", [128, 4, 256], F32)
        pT = sb("pT", [128, 2, 512], BF16)
        sgb = [sb(f"sg{i}", [128, 512], F32) for i in range(2)]
        sgt = [sb(f"sgt{i}", [128, 640], F32) for i in range(2)]

        ebuf = scrA[:, 0:4096].bitcast(BF16)
        cy = scrA[:, 4096:6144].bitcast(BF16)
        xl = scrA[:, 0:3072]
        hst = scrA[:, 3072:6144]
        ost = scrA

        if os.environ.get("KVERBOSE"):
            print("SBUF_REMAINING", nc.sbuf_bytes_remaining)
        MM = [ps(f"mm{i}") for i in range(3)]
        ST = [ps(f"st{i}") for i in range(2)]
        SC = [ps(f"sc{i}") for i in range(2)]
        PV = ps("pv")

        mm_i = [0]

        def next_mm():
            i = mm_i[0] % 3
            mm_i[0] += 1
            return MM[i], ("ps", "mm", i)

        eng_of = {"pe": "tensor", "act": "scalar", "dve": "vector", "pool": "gpsimd", "sp": "sync"}

        def mm(out, lhsT, rhs, start, stop, reads, writes, inc, tp=None):
            if tp is None:
                T.op("pe", lambda e: e.matmul(out, lhsT, rhs, start=start, stop=stop), reads, writes, inc)
            else:
                T.op("pe", lambda e: e.matmul(out, lhsT, rhs, start=start, stop=stop, tile_position=tp),
                     reads, writes, inc)

        def tr(out, in_, reads, writes, inc):
            T.op("pe", lambda e: e.transpose(out, in_, identf[:]), list(reads) + ["identf"], writes, inc)

        def act(out, in_, func, reads, writes, bias=None, scale=None):
            kw = {}
            if bias is not None:
                kw["bias"] = bias
            if scale is not None:
                kw["scale"] = scale
            T.op("act", lambda e: e.activation(out=out, in_=in_, func=func, **kw), reads, writes)

        def tt(eng, out, in0, in1, op, reads, writes):
            T.op(eng, lambda e: e.tensor_tensor(out=out, in0=in0, in1=in1, op=op), reads, writes)

        def ts(eng, out, in0, s1, op0, reads, writes, s2=None, op1=None):
            if op1 is None:
                T.op(eng, lambda e: e.tensor_scalar(out=out, in0=in0, scalar1=s1, scalar2=None, op0=op0),
                     reads, writes)
            else:
                T.op(eng, lambda e: e.tensor_scalar(out=out, in0=in0, scalar1=s1, scalar2=s2, op0=op0, op1=op1),
                     reads, writes)

        def stt(eng, out, in0, scalar, in1, op0, op1, reads, writes):
            T.op(eng, lambda e: e.scalar_tensor_tensor(out=out, in0=in0, scalar=scalar, in1=in1, op0=op0, op1=op1),
                 reads, writes)

        def cp(eng, out, in_, reads, writes):
            T.op(eng, lambda e: e.tensor_copy(out=out, in_=in_), reads, writes)

        def recip(out, in_, reads, writes):
            T.op("dve", lambda e: e.reciprocal(out=out, in_=in_), reads, writes)

        def memset(eng, ap, val, writes):
            T.op(eng, lambda e: e.memset(ap, val), (), writes)

        def dma(eng, sem, out, in_, reads, writes, mld=None):
            if mld is None:
                T.dma(eng, sem, lambda e: e.dma_start(out=out, in_=in_), reads, writes)
            else:
                T.dma(eng, sem, lambda e: e.dma_start(out=out, in_=in_, max_dma_last_dim=mld), reads, writes)

        def vcol(name, idx):
            o = VOFF[name] + idx
            return vecs[:, o:o + 1]

        def layer_tiles(l):
            out = []
            for n in range(NCH):
                out.append((wbin_d[l, n], win_d[l, n, :, 0:KC, :].rearrange("p k c -> p (k c)"), ("wb", l, "i", n)))
            for dc in range(16):
                out.append((wbout_d[l, dc], wout_d[l, dc, :, 0:KC, :].rearrange("p k c -> p (k c)"), ("wb", l, "o", dc)))
            for dc in range(16):
                out.append((wbpg_d[l, dc], wpg_d[l, dc, :, 0:KC, :].rearrange("p k c -> p (k c)"), ("wb", l, "g", dc)))
            return out
        LT = {l: layer_tiles(l) for l in layers}
        NT = NCH + 32
        wseq = []
        for s in range(nseq):
            for l in layers:
                for g in range(NG):
                    wseq.extend(LT[l])
        w_issued = [0]
        w_used = [0]
        cv_i = [0]
        NCV = 12

        def convert(l, i0, i1):
            for i in range(i0, min(i1, NT)):
                dst, src, key = LT[l][i]
                cs = cv_i[0] % NCV
                cv_i[0] += 1
                dma("pool", f"cv{cs}", dst, src, (), [key, ("cvsem", cs)], mld=512)

        def issue_w():
            i = w_issued[0]
            if i >= len(wseq):
                return
            sl = i % NWS
            src, _, key = wseq[i]
            dma("pool", f"w{sl}", wsl[sl][:].rearrange("p k c -> p (k c)"), src, [key], [("w", sl)])
            w_issued[0] += 1

        def next_w():
            i = w_used[0]
            while w_issued[0] < min(len(wseq), i + NWS):
                issue_w()
            w_used[0] += 1
            sl = i % NWS
            return wsl[sl], ("w", sl)

        dma("sp", "c0", identf[:], ident_d, (), ["identf"])
        dma("sp", "c1", vecs[:], vecs_d, (), ["vecs"])
        cp("dve", identb[:], identf[:], ["identf"], ["identb"])
        memset("dve", onesb[:], 1.0, ["onesb"])
        memset("dve", epsc[:], EPS, ["epsc"])
        so = VOFF["sink"]
        act(esink[:], vecs[:, so:so + NL * 8], AF.Exp, ["vecs"], ["esink"])
        m0, m0k, m0s = next_hs()
        m1, m1k, m1s = next_hs()
        dma("sp", m0s, m0[:, 0:384], mask_d[0], (), [m0k])
        dma("sp", m1s, m1[:, 0:384], mask_d[1], (), [m1k])
        for h in range(16):
            bt, btk, bts = hs[2 + h % 2], ("hs", 2 + h % 2), f"hs{2 + h % 2}"
            dma("sp", bts, bt[:, 0:384], bias_d[h], (), [btk])
            tt("dve", tf[h % 2][:, 0:384], bt[:, 0:384], m0[:, 0:384], ALU.mult,
               [btk, m0k], [("tf", h % 2)])
            tt("dve", biasT[:, h, :], tf[h % 2][:, 0:384], m1[:, 0:384], ALU.add,
               [("tf", h % 2), m1k], ["biasT"])
        T.barrier()

        def stats_step(s, tok0, n, kc):
            pieces = [(0, min(n, 512))] + ([(512, n)] if n > 512 else [])
            sl = kc % 2
            ht, hk, hsn = next_hs()
            dma("sp", hsn, ht[:, 0:n], hT_d[s, kc * 128:(kc + 1) * 128, tok0:tok0 + n],
                [("hT", s, kc)], [hk])
            act(sqb[sl][:, 0:n], ht[:, 0:n], AF.Square, [hk], [("sqb", sl)])
            for pi, (a, b) in enumerate(pieces):
                mm(ST[pi][:, 0:b - a], onesb[:], sqb[sl][:, a:b], kc == 0, kc == KC - 1,
                   ["onesb", ("sqb", sl)], [("ps", "st", pi)], inc=True)

        def stats_finish(n, dst, dkey):
            pieces = [(0, min(n, 512))] + ([(512, n)] if n > 512 else [])
            for pi, (a, b) in enumerate(pieces):
                act(tmp640[:, a:b], ST[pi][:, 0:b - a], AF.Sqrt, [("ps", "st", pi)], ["tmp640"],
                    bias=epsc[:, 0:1], scale=1.0 / D)
            recip(dst[:, 0:n], tmp640[:, 0:n], ["tmp640"], [dkey])

        def rms_stats(s, tok0, n, scale_inv):
            for kc in range(KC):
                stats_step(s, tok0, n, kc)
            stats_finish(n, rstd, "rstd")

        hbf = hb[:].rearrange("p k t -> p (k t)").bitcast(F32)

        def stage0_real(s):
            for tg in range(S // 128):
                par = tg % 2
                xbuf = scrA[:, par * 2048:(par + 1) * 2048]
                stg = hbf[:, par * 2048:(par + 1) * 2048].rearrange("p (k t) -> p k t", k=KC)
                dma("sp", f"x{par}", xbuf, x_d[s, tg * 128:(tg + 1) * 128, :], (), [("xb", par)])
                for k4 in range(4):
                    bank, bkey = next_mm()
                    for kk in range(4):
                        kc = k4 * 4 + kk
                        tr(bank[:, kk * 128:(kk + 1) * 128], xbuf[:, kc * 128:(kc + 1) * 128],
                           [("xb", par)], [bkey], inc=(kk == 3))
                    eng = "dve" if k4 % 2 == 0 else "act"
                    src = bank[:].rearrange("p (k t) -> p k t", k=4)
                    if eng == "dve":
                        cp("dve", stg[:, k4 * 4:(k4 + 1) * 4, :], src, [bkey], [("stg", par)])
                    else:
                        act(stg[:, k4 * 4:(k4 + 1) * 4, :], src, AF.Copy, [bkey], [("stg", par)])
                dst = hT_d[s].rearrange("(k p) t -> p k t", p=128)[:, :, tg * 128:(tg + 1) * 128]
                dma("sp", f"stg{par}", dst, stg, [("stg", par)], [("hT", s, k) for k in range(KC)])

        def epilogue(s):
            fo = VOFF["final_g"]
            for tq in range(S // 256):
                tok0 = tq * 256
                rms_stats(s, tok0, 256, 1.0 / D)
                if os.environ.get("KSTOP", "") == "rms":
                    return
                ostv = scrA[:, 0:4096].rearrange("p (b d) -> p b d", b=2)
                for kc in range(KC):
                    sl = kc % 2
                    ht, hk, hsn = next_hs()
                    dma("sp", hsn, ht[:, 0:256], hT_d[s, kc * 128:(kc + 1) * 128, tok0:tok0 + 256],
                        [("hT", s, kc)], [hk])
                    stt("dve", tf[sl][:, 0:256], ht[:, 0:256], vecs[:, fo + kc:fo + kc + 1], rstd[:, 0:256],
                        ALU.mult, ALU.mult, [hk, "vecs", "rstd"], [("tf", sl)])
                    bank, bkey = next_mm()
                    for tb in range(2):
                        tr(bank[:, tb * 128:(tb + 1) * 128], tf[sl][:, tb * 128:(tb + 1) * 128],
                           [("tf", sl)], [bkey], inc=(tb == 1))
                    src = bank[:, 0:256].rearrange("p (b d) -> p b d", b=2)
                    if kc % 2 == 0:
                        act(ostv[:, :, kc * 128:(kc + 1) * 128], src, AF.Copy, [bkey], ["ost"])
                    else:
                        cp("dve", ostv[:, :, kc * 128:(kc + 1) * 128], src, [bkey], ["ost"])
                for b in range(2):
                    dma("sp", f"ost{b}", out_d[s, tok0 + b * 128:tok0 + (b + 1) * 128, :], ostv[:, b, :],
                        ["ost"], [("out", s)])

        pending_stats = [None]

        def layer_group(s, l, g, nxt=None):
            T0 = g * G
            Wn = min(640, S - T0)
            KST = os.environ.get("KSTOP", "")
            ebv = ebuf.rearrange("p (k t) -> p k t", k=16)
            cyv = cy.rearrange("p (k t) -> p k t", k=8)
            if pending_stats[0] == (s, l, g):
                rs, rsk = rsN, "rsN"
            else:
                rms_stats(s, T0, Wn, 1.0 / D)
                rs, rsk = rstd, "rstd"
            go = VOFF["norm_g"] + l * 16
            for kc in range(KC):
                sl = kc % 2
                ht, hk, hsn = next_hs()
                dma("sp", hsn, ht[:, 0:Wn], hT_d[s, kc * 128:(kc + 1) * 128, T0:T0 + Wn],
                    [("hT", s, kc)], [hk])
                stt("dve", hb[:, kc, 0:Wn], ht[:, 0:Wn], vecs[:, go + kc:go + kc + 1], rs[:, 0:Wn],
                    ALU.mult, ALU.mult, [hk, "vecs", rsk], [("hb", kc)])
            if KST == "hb":
                return
            akeys = [("a", c) for c in range(8)]
            if g == 0:
                memset("pool", abuf[:, :, 0:16], 0.0, akeys)
            else:
                T.op("pool", lambda e: e.tensor_copy(out=abuf[:, :, 0:144], in_=abuf[:, :, 512:656]), akeys, akeys)
            if g == NG - 1:
                memset("pool", abuf[:, :, 528:544], 0.0, akeys)
            if g == 0:
                lead = [(0, 512), (512, 640)]
            elif g == NG - 1:
                lead = [(128, 512)]
            else:
                lead = [(128, 640)]
            main = [(0, 512)]
            hbk = [("hb", kc) for kc in range(KC)]
            for n, (typ, ci) in enumerate(CH):
                if n >= int(os.environ.get("KNCH", 1000)):
                    return
                w, wkey = next_w()
                if os.environ.get("KAFTERW"):
                    return
                if typ == "v":
                    blocks = []
                    for (a, b) in lead:
                        blocks += list(range(a // 128, b // 128))
                    for b0 in range(0, len(blocks), 4):
                        bl = blocks[b0:b0 + 4]
                        bank, bkey = next_mm()
                        for bi, blk in enumerate(bl):
                            for kc in range(KC):
                                mm(bank[:, bi * 128:(bi + 1) * 128], hb[:, kc, blk * 128:(blk + 1) * 128], w[:, kc, :],
                                   kc == 0, kc == KC - 1, [("hb", kc), wkey], [bkey],
                                   inc=(kc == KC - 1 and bi == len(bl) - 1))
                        for bi, blk in enumerate(bl):
                            KB = (T0 + blk * 128) // 128
                            ks = KB % 6
                            cp("dve", vbuf[:, ks, ci * 128:(ci + 1) * 128], bank[:, bi * 128:(bi + 1) * 128],
                               [bkey], [("v", ks)])
                    continue
                pieces = lead if typ in ("k", "ag", "av") else main
                for (a, b) in pieces:
                    nt = b - a
                    bank, bkey = next_mm()
                    for kc in range(KC):
                        mm(bank[:, 0:nt], w[:, kc, :], hb[:, kc, a:b], kc == 0, kc == KC - 1,
                           [("hb", kc), wkey], [bkey], inc=(kc == KC - 1))
                    if os.environ.get("KNOEVAC"):
                        continue
                    if typ == "k":
                        for blk in range(a // 128, b // 128):
                            KB = (T0 + blk * 128) // 128
                            ks = KB % 6
                            o0 = blk * 128 - a
                            if True:
                                cp("dve", kbuf[:, ci, ks, :], bank[:, o0:o0 + 128], [bkey], [("k", ci, ks)])
                            else:
                                act(kbuf[:, ci, ks, :], bank[:, o0:o0 + 128], AF.Copy, [bkey], [("k", ci, ks)])
                    elif typ == "ag":
                        act(sgt[ci % 2][:, a:b], bank[:, 0:nt], AF.Sigmoid, [bkey], [("sgt", ci % 2)])
                    elif typ == "av":
                        tt("dve", abuf[:, ci, 16 + a:16 + b], bank[:, 0:nt], sgt[ci % 2][:, a:b], ALU.mult,
                           [bkey, ("sgt", ci % 2)], [("a", ci)])
                    elif typ == "az":
                        act(ybuf[:, ci, :], bank[:, 0:nt], AF.Silu, [bkey], [("y", ci)])
                    elif typ == "bz":
                        act(ybuf[:, 8 + ci, :], bank[:, 0:nt], AF.Silu, [bkey], [("y", 8 + ci)])
                    elif typ == "q":
                        ts("dve", qbuf[:, ci, :], bank[:, 0:nt], 0.125, ALU.mult, [bkey], [("q", ci)])
            if KST == "inproj":
                return
            cwo = VOFF["conv_w"] + l * 8 * 31
            for c in range(8):
                ds = c % 2
                idb = identb[:]
                cwv = vecs[:, cwo + c * 31:cwo + c * 31 + 31]
                tt(os.environ.get("KDGENG", "dve"), dg[ds][:], bass.AP(idb.tensor, idb.offset, [idb.ap[0], [0, 31], idb.ap[1]]),
                   bass.AP(cwv.tensor, cwv.offset, [cwv.ap[0], cwv.ap[1], [0, 128]]), ALU.mult,
                   ["identb", "vecs"], [("dg", ds)])
                for k in range(31):
                    mm(SC[ds][:, :], dg[ds][:, k, :], abuf[:, c, k + 1:k + 1 + 512], k == 0, k == 30,
                       [("dg", ds), ("a", c)], [("ps", "sc", ds)], inc=(k == 30))
                act(cyv[:, c, :], SC[ds][:, :], AF.Identity, [("ps", "sc", ds)], [("cy", c)],
                    bias=vcol("conv_b", l * 8 + c))
                tt("dve", ysq[ds][:], cyv[:, c, :], cyv[:, c, :], ALU.mult, [("cy", c)], [("ysq", ds)])
                mm(ST[0][:, :], onesb[:], cyv[:, c, :], c == 0, c == 7, ["onesb", ("cy", c)], [("ps", "st", 0)], inc=True)
                mm(ST[1][:, :], onesb[:], ysq[ds][:], c == 0, c == 7, ["onesb", ("ysq", ds)], [("ps", "st", 1)], inc=True)
            ts("dve", meant[:], ST[0][:, :], 1.0 / 1024, ALU.mult, [("ps", "st", 0)], ["meant"])
            tt("dve", tmp640[:, 0:512], meant[:], meant[:], ALU.mult, ["meant"], ["tmp640"])
            stt("dve", tmp640[:, 0:512], ST[1][:, :], 1.0 / 1024, tmp640[:, 0:512], ALU.mult, ALU.subtract,
                [("ps", "st", 1), "tmp640"], ["tmp640"])
            act(tmp640[:, 0:512], tmp640[:, 0:512], AF.Sqrt, ["tmp640"], ["tmp640"], bias=epsc[:, 0:1], scale=1.0)
            recip(lnA[:], tmp640[:, 0:512], ["tmp640"], ["lnA"])
            stt("dve", lnB[:], meant[:], -1.0, lnA[:], ALU.mult, ALU.mult, ["meant", "lnA"], ["lnB"])
            for c in range(8):
                sl = c % 2
                tt("dve", tf[sl][:], cyv[:, c, :], lnA[:], ALU.mult, [("cy", c), "lnA"], [("tf", sl)])
                tt("dve", tf[sl][:], tf[sl][:], lnB[:], ALU.add, [("tf", sl), "lnB"], [("tf", sl)])
                act(tb16[sl][:], tf[sl][:], AF.Silu, [("tf", sl), "vecs"], [("tb16", sl)],
                    bias=vcol("cln_b", l * 8 + c), scale=vcol("cln_g", l * 8 + c))
                tt("pool", ybuf[:, c, :], tb16[sl][:], ybuf[:, c, :], ALU.mult, [("tb16", sl), ("y", c)], [("y", c)])
            if KST == "conv":
                return
            it = 0
            for qb in range(4):
                Q = 4 * g + qb
                kbs = [kb for kb in range(3) if 0 <= Q + kb - 1 < S // 128]
                for j in range(4):
                    pts = []
                    for kb in kbs:
                        ks = (Q + kb - 1) % 6
                        scb = it % 2
                        pti = it % 6
                        it += 1
                        sck = ("ps", "sc", scb)
                        for hh in range(4):
                            c = 2 * j + hh // 2
                            hf = hh % 2
                            mm(SC[scb][:, hh * 128:(hh + 1) * 128], identb[:],
                               biasT[:, 4 * j + hh, kb * 128:(kb + 1) * 128],
                               True, False, ["identb", "biasT"], [sck], inc=False)
                            mm(SC[scb][:, hh * 128:(hh + 1) * 128], kbuf[hf * 64:(hf + 1) * 64, j, ks, :],
                               qbuf[hf * 64:(hf + 1) * 64, c, qb * 128:(qb + 1) * 128], False, True,
                               [("k", j, ks), ("q", c)], [sck], inc=(hh == 3))
                        act(PT[pti][:], SC[scb][:, :], AF.Exp, [sck], [("pt", pti)])
                        pts.append((pti, ks))
                    pvk = ("ps", "pv")
                    if os.environ.get("KATT") == "s":
                        continue
                    for cc in range(2):
                        for hf in range(2):
                            hh = 2 * cc + hf
                            for i, (pti, ks) in enumerate(pts):
                                mm(PV[hf * 64:(hf + 1) * 64, cc * 128:(cc + 1) * 128], vbuf[:, ks, j * 64:(j + 1) * 64],
                                   PT[pti][:, hh * 128:(hh + 1) * 128], i == 0, i == len(pts) - 1,
                                   [("v", ks), ("pt", pti)], [pvk], inc=False, tp=((0, 64) if hf else None))
                            for i, (pti, ks) in enumerate(pts):
                                mm(PV[hf * 64:(hf + 1) * 64, 256 + cc * 128:256 + (cc + 1) * 128], onesb[:, 0:64],
                                   PT[pti][:, hh * 128:(hh + 1) * 128], i == 0, i == len(pts) - 1,
                                   ["onesb", ("pt", pti)], [pvk],
                                   inc=(cc == 1 and hf == 1 and i == len(pts) - 1), tp=((0, 64) if hf else None))
                    dsl = (qb * 4 + j) % 2
                    for cc in range(2):
                        ts("dve", dens[dsl][:, cc * 128:(cc + 1) * 128], PV[:, 256 + cc * 128:256 + (cc + 1) * 128],
                           esink[:, l * 8 + j * 2 + cc:l * 8 + j * 2 + cc + 1], ALU.add, [pvk, "esink"], [("dens", dsl)])
                    recip(dens[dsl][:], dens[dsl][:], [("dens", dsl)], [("dens", dsl)])
                    yv = ybuf[:, 8 + 2 * j:8 + 2 * j + 2, qb * 128:(qb + 1) * 128]
                    yk = [("y", 8 + 2 * j), ("y", 8 + 2 * j + 1)]
                    tt("dve", rz[dsl][:].rearrange("p (c q) -> p c q", c=2), dens[dsl][:].rearrange("p (c q) -> p c q", c=2),
                       yv, ALU.mult, [("dens", dsl)] + yk, [("rz", dsl)])
                    tt("dve", yv, PV[:, 0:256].rearrange("p (c q) -> p c q", c=2),
                       rz[dsl][:].rearrange("p (c q) -> p c q", c=2), ALU.mult, [pvk, ("rz", dsl)], yk)
            if KST == "attn":
                return
            LA = 2
            pre = {}

            def pf(dc):
                ht, hk, hsn = next_hs()
                dma("sp", hsn, ht[:, 0:G], hT_d[s, dc * 128:(dc + 1) * 128, T0:T0 + G], [("hT", s, dc)], [hk])
                pre[dc] = (ht, hk, hsn)
            for dc in range(LA):
                pf(dc)
            if nxt is not None:
                T0n = nxt[2] * G
                Wnn = min(640, S - T0n)
            for dc in range(16):
                w, wkey = next_w()
                bank, bkey = next_mm()
                for kc in range(KC):
                    mm(bank[:, :], w[:, kc, :], ybuf[:, kc, :], kc == 0, kc == KC - 1, [("y", kc), wkey], [bkey],
                       inc=(kc == KC - 1))
                if dc + LA < 16:
                    pf(dc + LA)
                if nxt is not None:
                    stats_step(nxt[0], T0n, Wnn, dc)
                ht, hk, hsn = pre.pop(dc)
                tt("dve", ht[:, 0:G], ht[:, 0:G], bank[:, :], ALU.add, [hk, bkey], [hk])
                act(hb[:, dc, 0:G], ht[:, 0:G], AF.Copy, [hk], [("hb", dc)])
                dma("sp", hsn, hT_d[s, dc * 128:(dc + 1) * 128, T0:T0 + G], ht[:, 0:G], [hk], [("hT", s, dc)])
            if nxt is not None:
                stats_finish(Wnn, rsN, "rsN")
                pending_stats[0] = nxt
            if KST == "outproj":
                return
            dma("sp", "ptile", ptile[:], p_d[l, s, T0:T0 + G, :].rearrange("(b p) f -> p b f", p=128), (), ["ptile"])
            for pc in range(2):
                for tb in range(4):
                    tr(ST[pc][:, tb * 128:(tb + 1) * 128], ptile[:, tb, pc * 128:(pc + 1) * 128], ["ptile"],
                       [("ps", "st", pc)], inc=(tb == 3))
                cp("dve", pT[:, pc, :], ST[pc][:, :], [("ps", "st", pc)], [("pT", pc)])
            for dc in range(16):
                bank, bkey = next_mm()
                for pc in range(2):
                    mm(bank[:, :], wpe[:, pc, dc * 128:(dc + 1) * 128], pT[:, pc, :], pc == 0, pc == 1,
                       ["wpe", ("pT", pc)], [bkey], inc=(pc == 1))
                sl = dc % 2
                act(ebv[:, dc, :], bank[:, :], AF.Copy, [bkey], [("e", dc)])
                tt("dve", ysq[sl][:], ebv[:, dc, :], ebv[:, dc, :], ALU.mult, [("e", dc)], [("ysq", sl)])
                mm(ST[0][:, :], onesb[:], ysq[sl][:], dc == 0, dc == 15, ["onesb", ("ysq", sl)], [("ps", "st", 0)], inc=True)
            act(tmp640[:, 0:512], ST[0][:, :], AF.Sqrt, [("ps", "st", 0)], ["tmp640"], bias=epsc[:, 0:1], scale=1.0 / D)
            recip(rstd[:, 0:512], tmp640[:, 0:512], ["tmp640"], ["rstd"])
            pgo = VOFF["pe_g"] + l * 16
            pre2 = {}

            def pf2(dc):
                ht, hk, hsn = next_hs()
                dma("sp", hsn, ht[:, 0:G], hT_d[s, dc * 128:(dc + 1) * 128, T0:T0 + G], [("hT", s, dc)], [hk])
                pre2[dc] = (ht, hk, hsn)
            for dc in range(LA):
                pf2(dc)
            for dc in range(16):
                w, wkey = next_w()
                bank, bkey = next_mm()
                for kc in range(KC):
                    mm(bank[:, :], w[:, kc, :], hb[:, kc, 0:G], kc == 0, kc == KC - 1, [("hb", kc), wkey], [bkey],
                       inc=(kc == KC - 1))
                if dc + LA < 16:
                    pf2(dc + LA)
                sl = dc % 2
                act(sgb[sl][:], bank[:, :], AF.Sigmoid, [bkey], [("sg", sl)])
                tt("dve", tf[sl][:], ebv[:, dc, :], rstd[:, 0:512], ALU.mult, [("e", dc), "rstd"], [("tf", sl)])
                stt("dve", tf[sl][:], tf[sl][:], vecs[:, pgo + dc:pgo + dc + 1], sgb[sl][:], ALU.mult, ALU.mult,
                    [("tf", sl), "vecs", ("sg", sl)], [("tf", sl)])
                ht, hk, hsn = pre2.pop(dc)
                tt("dve", ht[:, 0:G], ht[:, 0:G], tf[sl][:], ALU.add, [hk, ("tf", sl)], [hk])
                dma("sp", hsn, hT_d[s, dc * 128:(dc + 1) * 128, T0:T0 + G], ht[:, 0:G], [hk], [("hT", s, dc)])

        KSTOP = os.environ.get("KSTOP", "")
        convert(layers[0], 0, NT) if len(layers) else None
        for s in range(nseq):
            if KSTOP == "setup":
                break
            if do_stage0:
                stage0_real(s)
                T.barrier()
            if KSTOP == "stage0":
                break
            for l in layers:
                for pc in range(2):
                    if os.environ.get("KNOWPE"):
                        continue
                    dma("pool", f"wpe{pc}", wpe[:, pc, :], wpe_d[l, :, pc, :], (), ["wpe"], mld=int(os.environ.get("KMLD", 512)))
                for g in range(int(os.environ.get("KGROUPS", NG))):
                    li = layers.index(l)
                    if s == 0 and li + 1 < len(layers):
                        convert(layers[li + 1], g * 20, (g + 1) * 20)
                    ng = int(os.environ.get("KGROUPS", NG))
                    if g + 1 < ng:
                        nxt = (s, l, g + 1)
                    elif li + 1 < len(layers) and ng == NG:
                        nxt = (s, layers[li + 1], 0)
                    else:
                        nxt = None
                    layer_group(s, l, g, nxt)
            T.barrier()
            if do_final:
                epilogue(s)
                T.barrier()
        T.barrier()

        names = T.sem_names()
        sems = {n: es.enter_context(nc.semaphore("s_" + n)) for n in names}
        with nc.Block() as block:
            def emit(kind):
                def run(e):
                    for waits, fn, inc in T.ops[kind]:
                        for sname, v in waits:
                            e.wait_ge(sems[sname], v)
                        ins = fn(e)
                        if inc is not None:
                            ins.then_inc(sems[inc[0]], inc[1])
                    for sname, v in T.pending[kind].items():
                        e.wait_ge(sems[sname], v)
                return run
            block.tensor(emit("pe"))
            block.scalar(emit("act"))
            block.vector(emit("dve"))
            block.gpsimd(emit("pool"))
            block.sync(emit("sp"))
    return nc


def prep_shared(norm_g, w_in, conv_w, conv_b, cln_g, cln_b, sink, rel_bias, w_out, w_pe, pe_g, w_pg, final_g):
    f = lambda a: np.ascontiguousarray(np.asarray(a, dtype=np.float32))
    w_in, w_out, w_pe, w_pg = f(w_in), f(w_out), f(w_pe), f(w_pg)
    idx = _chunk_cols()
    wi = w_in[:, :, idx.reshape(-1)].reshape(NL, KC, 128, NCH, 128)
    def padk(a):
        o = np.zeros(a.shape[:3] + (KC + 1, 128), np.float32)
        o[:, :, :, :KC, :] = a
        return o
    wi = padk(wi.transpose(0, 3, 2, 1, 4))
    wo = padk(w_out.reshape(NL, KC, 128, 16, 128).transpose(0, 3, 2, 1, 4))
    wg = padk(w_pg.reshape(NL, KC, 128, 16, 128).transpose(0, 3, 2, 1, 4))
    wp = np.ascontiguousarray(w_pe.reshape(NL, 2, 128, D).transpose(0, 2, 1, 3))
    vecs = _pack_vecs(f(norm_g), f(pe_g), f(final_g), f(conv_b), f(cln_g), f(cln_b), f(conv_w), f(sink))
    buckets, band = _t5_band_buckets()
    gathered = f(rel_bias)[buckets]
    biasg = np.ascontiguousarray(gathered.reshape(128, 3, 128, 16).transpose(3, 2, 1, 0)).reshape(16, 128, 384)
    m01 = band.astype(np.float32).reshape(128, 3, 128).transpose(2, 1, 0).reshape(128, 384)
    masks = np.ascontiguousarray(np.stack([m01, (m01 - 1.0) * 30000.0]).astype(np.float32))
    return {"w_in": wi, "w_out": wo, "w_pg": wg, "w_pe": wp, "vecs": vecs, "biasg": biasg, "masks": masks,
            "ident": np.eye(128, dtype=np.float32)}


def kernel(x, p, norm_g, w_in, conv_w, conv_b, cln_g, cln_b, sink, rel_bias, w_out, w_pe, pe_g, w_pg, final_g):
    x = np.asarray(x, dtype=np.float32)
    p = np.asarray(p, dtype=np.float32)
    shared = prep_shared(norm_g, w_in, conv_w, conv_b, cln_g, cln_b, sink, rel_bias, w_out, w_pe, pe_g, w_pg, final_g)
    nc = build_program()
    in_maps = []
    for c in range(NCORES):
        m = dict(shared)
        m["x"] = np.ascontiguousarray(x[c * SEQ_PER_CORE:(c + 1) * SEQ_PER_CORE])
        m["p"] = np.ascontiguousarray(p[:, c * SEQ_PER_CORE:(c + 1) * SEQ_PER_CORE])
        in_maps.append(m)
    res = run_bass_kernel_spmd(nc, in_maps, core_ids=list(range(NCORES)))
    out = np.concatenate([r["out"] for r in res.results], axis=0)
    return out.astype(np.float32)
```

```python
import os
import numpy as np
import concourse.bass as bass
import concourse.mybir as mybir
from concourse.bass_utils import run_bass_kernel_spmd

F32 = mybir.dt.float32
BF16 = mybir.dt.bfloat16
AF = mybir.ActivationFunctionType
ALU = mybir.AluOpType

D = 2048
S = 2048
NL = 4
G = 512
NG = S // G
KC = 16
EPS = 1e-6
NCORES = 8
SEQ_PER_CORE = 2
NWS = int(os.environ.get("KNWS", 3))

CH = []
for _j in range(4):
    CH.append(("k", _j))
for _vc in range(2):
    CH.append(("v", _vc))
for _c in range(8):
    CH.append(("ag", _c))
    CH.append(("av", _c))
for _c in range(8):
    CH.append(("az", _c))
for _c in range(8):
    CH.append(("q", _c))
for _c in range(8):
    CH.append(("bz", _c))
NCH = len(CH)


def _chunk_cols():
    AV0, AG0, AZ0, Q0, K0, V0, BZ0 = 0, 1024, 2048, 3072, 4096, 4352, 4608
    idx = np.zeros((NCH, 128), np.int64)
    r = np.arange(128)
    for n, (t, i) in enumerate(CH):
        if t == "k":
            idx[n] = K0 + i * 64 + (r % 64)
        elif t == "v":
            idx[n] = V0 + i * 128 + r
        elif t == "ag":
            idx[n] = AG0 + i * 128 + r
        elif t == "av":
            idx[n] = AV0 + i * 128 + r
        elif t == "az":
            idx[n] = AZ0 + i * 128 + r
        elif t == "q":
            idx[n] = Q0 + i * 128 + r
        elif t == "bz":
            idx[n] = BZ0 + i * 128 + r
    return idx


VOFF = {}
_o = 0
for _name, _n in (("norm_g", NL * 16), ("pe_g", NL * 16), ("final_g", 16), ("conv_b", NL * 8),
                  ("cln_g", NL * 8), ("cln_b", NL * 8), ("conv_w", NL * 8 * 31), ("sink", NL * 8)):
    VOFF[_name] = _o
    _o += _n
NV = _o


def _pack_vecs(norm_g, pe_g, final_g, conv_b, cln_g, cln_b, conv_w, sink):
    v = np.zeros((128, NV), np.float32)
    def put(name, arr):
        v[:, VOFF[name]:VOFF[name] + arr.shape[1]] = arr
    put("norm_g", norm_g.reshape(NL, 16, 128).transpose(2, 0, 1).reshape(128, -1))
    put("pe_g", pe_g.reshape(NL, 16, 128).transpose(2, 0, 1).reshape(128, -1))
    put("final_g", final_g.reshape(16, 128).T)
    put("conv_b", conv_b.reshape(NL, 8, 128).transpose(2, 0, 1).reshape(128, -1))
    put("cln_g", cln_g.reshape(NL, 8, 128).transpose(2, 0, 1).reshape(128, -1))
    put("cln_b", cln_b.reshape(NL, 8, 128).transpose(2, 0, 1).reshape(128, -1))
    put("conv_w", conv_w.reshape(NL, 31, 8, 128).transpose(3, 0, 2, 1).reshape(128, -1))
    sk = np.zeros((128, NL, 4, 2), np.float32)
    for j in range(4):
        for cc in range(2):
            sk[:64, :, j, cc] = sink[:, 4 * j + 2 * cc][None, :]
            sk[64:, :, j, cc] = sink[:, 4 * j + 2 * cc + 1][None, :]
    put("sink", sk.reshape(128, -1))
    return v


def _t5_band_buckets():
    BLK, NUM_BUCKETS, MAX_DISTANCE, WINDOW = 128, 32, 128, 128
    q_off = np.arange(BLK)[:, None]
    k_off = np.arange(3 * BLK)[None, :] - BLK
    rel = k_off - q_off
    half = NUM_BUCKETS // 2
    ret = (rel > 0).astype(np.int32) * half
    n = np.abs(rel)
    max_exact = half // 2
    large = max_exact + (np.log(np.maximum(n, 1) / max_exact)
                         / np.log(MAX_DISTANCE / max_exact) * (half - max_exact)).astype(np.int32)
    large = np.minimum(large, half - 1)
    ret = ret + np.where(n < max_exact, n, large)
    return ret.astype(np.int32), (n <= WINDOW)


class Tracker:
    ENG = ("pe", "act", "dve", "pool", "sp")

    def __init__(self):
        self.ops = {e: [] for e in self.ENG}
        self.cnt = {e: 0 for e in self.ENG}
        self.dcnt = {}
        self.last_w = {}
        self.rd = {}
        self.seen = {e: {} for e in self.ENG}
        self.pending = {e: {} for e in self.ENG}

    def _deps(self, eng, reads, writes):
        deps = dict(self.pending[eng])
        self.pending[eng] = {}

        def add(ev):
            if ev is None:
                return
            s, v = ev
            if deps.get(s, 0) < v:
                deps[s] = v
        for k in reads:
            add(self.last_w.get(k))
        for k in writes:
            add(self.last_w.get(k))
            for ev in self.rd.get(k, ()):
                add(ev)
        out = []
        seen = self.seen[eng]
        for s, v in deps.items():
            if s == "pe" and eng == "pe":
                continue
            if seen.get(s, 0) >= v:
                continue
            seen[s] = v
            out.append((s, v))
        return out

    def _record(self, ev, reads, writes):
        for k in writes:
            self.last_w[k] = ev
            self.rd[k] = []
        for k in reads:
            self.rd.setdefault(k, []).append(ev)

    def op(self, eng, fn, reads=(), writes=(), inc=True):
        waits = self._deps(eng, reads, writes)
        if inc:
            self.cnt[eng] += 1
            ev = (eng, self.cnt[eng])
        else:
            ev = (eng, self.cnt[eng] + 1)
        self._record(ev, reads, writes)
        self.ops[eng].append((waits, fn, (eng, 1) if inc else None))

    def dma(self, eng, sem, fn, reads=(), writes=()):
        waits = self._deps(eng, reads, writes)
        self.dcnt[sem] = self.dcnt.get(sem, 0) + 16
        ev = (sem, self.dcnt[sem])
        self._record(ev, reads, writes)
        self.ops[eng].append((waits, fn, (sem, 16)))

    def barrier(self):
        allv = dict(self.cnt)
        allv.update(self.dcnt)
        for e in self.ENG:
            for s, v in allv.items():
                if v > 0 and self.pending[e].get(s, 0) < v:
                    self.pending[e][s] = v

    def sem_names(self):
        return list(self.ENG) + sorted(self.dcnt.keys())


def build_program(nseq=SEQ_PER_CORE, layers=(0, 1, 2, 3), do_stage0=True, do_final=True, debug_h=False):
    nc = bass.Bass("TRN2", target_bir_lowering=False)
    T = Tracker()

    x_d = nc.dram_tensor("x", [nseq, S, D], F32, kind="ExternalInput").ap()
    p_d = nc.dram_tensor("p", [NL, nseq, S, 256], F32, kind="ExternalInput").ap()
    win_d = nc.dram_tensor("w_in", [NL, NCH, 128, KC + 1, 128], F32, kind="ExternalInput").ap()
    wout_d = nc.dram_tensor("w_out", [NL, 16, 128, KC + 1, 128], F32, kind="ExternalInput").ap()
    wpg_d = nc.dram_tensor("w_pg", [NL, 16, 128, KC + 1, 128], F32, kind="ExternalInput").ap()
    wpe_d = nc.dram_tensor("w_pe", [NL, 128, 2, D], F32, kind="ExternalInput").ap()
    vecs_d = nc.dram_tensor("vecs", [128, NV], F32, kind="ExternalInput").ap()
    bias_d = nc.dram_tensor("biasg", [16, 128, 384], F32, kind="ExternalInput").ap()
    mask_d = nc.dram_tensor("masks", [2, 128, 384], F32, kind="ExternalInput").ap()
    ident_d = nc.dram_tensor("ident", [128, 128], F32, kind="ExternalInput").ap()
    out_d = nc.dram_tensor("out", [nseq, S, D], F32, kind="ExternalOutput").ap()
    if debug_h:
        hT_t = nc.dram_tensor("hT", [nseq, D, S], F32, kind="ExternalOutput")
    else:
        hT_t = nc.dram_tensor("hT", [nseq, D, S], F32)
    hT_d = hT_t.ap()
    wbin_d = nc.dram_tensor("wb_in", [NL, NCH, 128, KC * 128], BF16).ap()
    wbout_d = nc.dram_tensor("wb_out", [NL, 16, 128, KC * 128], BF16).ap()
    wbpg_d = nc.dram_tensor("wb_pg", [NL, 16, 128, KC * 128], BF16).ap()

    from contextlib import ExitStack
    es = ExitStack()

    def sb(name, shape, dt):
        return es.enter_context(nc.sbuf_tensor(name, shape, dt))

    def ps(name):
        return es.enter_context(nc.psum_tensor(name, [128, 512], F32))

    with es:
        identf = sb("identf", [128, 128], F32)
        identb = sb("identb", [128, 128], BF16)
        onesb = sb("onesb", [128, 128], BF16)
        vecs = sb("vecs_sb", [128, NV], F32)
        esink = sb("esink", [128, NL * 8], F32)
        epsc = sb("epsc", [128, 1], F32)
        biasT = sb("biasT", [128, 16, 384], BF16)
        NHS = int(os.environ.get("KNHS", 7))
        hs = [sb(f"hs{i}", [128, 640], F32) for i in range(NHS)]
        hs_i = [0]

        def next_hs():
            i = hs_i[0] % NHS
            hs_i[0] += 1
            return hs[i], ("hs", i), f"hs{i}"
        sqb = [sb(f"sqb{i}", [128, 640], BF16) for i in range(2)]
        hb = sb("hb", [128, KC, 640], BF16)
        rstd = sb("rstd", [128, 640], F32)
        rsN = sb("rsN", [128, 640], F32)
        tmp640 = sb("tmp640", [128, 640], F32)
        wsl = [sb(f"w{i}", [128, KC, 128], BF16) for i in range(NWS)]
        wpe = sb("wpe", [128, 2, D], BF16)
        abuf = sb("abuf", [128, 8, 656], BF16)
        kbuf = sb("kbuf", [128, 4, 6, 128], BF16)
        vbuf = sb("vbuf", [128, 6, 256], BF16)
        qbuf = sb("qbuf", [128, 8, 512], BF16)
        ybuf = sb("ybuf", [128, 16, 512], BF16)
        dg = [sb(f"dg{i}", [128, 31, 128], BF16) for i in range(2)]
        scrA = sb("scrA", [128, 6144], F32)
        ysq = [sb(f"ysq{i}", [128, 512], BF16) for i in range(2)]
        meant = sb("meant", [128, 512], F32)
        lnA = sb("lnA", [128, 512], F32)
        lnB = sb("lnB", [128, 512], F32)
        tf = [sb(f"tf{i}", [128, 512], F32) for i in range(2)]
        tb16 = [sb(f"tb16_{i}", [128, 512], BF16) for i in range(2)]
        PT = [sb(f"pt{i}", [128, 512], BF16) for i in range(6)]
        dens = [sb(f"dens{i}", [128, 256], F32) for i in range(2)]
        rz = [sb(f"rz{i}", [128, 256], F32) for i in range(2)]
        ptile = sb("ptile", [128, 4, 256], F32)
        pT = sb("pT", [128, 2, 512], BF16)
        sgb = [sb(f"sg{i}", [128, 512], F32) for i in range(2)]
        sgt = [sb(f"sgt{i}", [128, 640], F32) for i in range(2)]

        ebuf = scrA[:, 0:4096].bitcast(BF16)
        cy = scrA[:, 4096:6144].bitcast(BF16)
        xl = scrA[:, 0:3072]
        hst = scrA[:, 3072:6144]
        ost = scrA

        if os.environ.get("KVERBOSE"):
            print("SBUF_REMAINING", nc.sbuf_bytes_remaining)
        MM = [ps(f"mm{i}") for i in range(3)]
        ST = [ps(f"st{i}") for i in range(2)]
        SC = [ps(f"sc{i}") for i in range(2)]
        PV = ps("pv")

        mm_i = [0]

        def next_mm():
            i = mm_i[0] % 3
            mm_i[0] += 1
            return MM[i], ("ps", "mm", i)

        eng_of = {"pe": "tensor", "act": "scalar", "dve": "vector", "pool": "gpsimd", "sp": "sync"}

        def mm(out, lhsT, rhs, start, stop, reads, writes, inc, tp=None):
            if tp is None:
                T.op("pe", lambda e: e.matmul(out, lhsT, rhs, start=start, stop=stop), reads, writes, inc)
            else:
                T.op("pe", lambda e: e.matmul(out, lhsT, rhs, start=start, stop=stop, tile_position=tp),
                     reads, writes, inc)

        def tr(out, in_, reads, writes, inc):
            T.op("pe", lambda e: e.transpose(out, in_, identf[:]), list(reads) + ["identf"], writes, inc)

        def act(out, in_, func, reads, writes, bias=None, scale=None):
            kw = {}
            if bias is not None:
                kw["bias"] = bias
            if scale is not None:
                kw["scale"] = scale
            T.op("act", lambda e: e.activation(out=out, in_=in_, func=func, **kw), reads, writes)

        def tt(eng, out, in0, in1, op, reads, writes):
            T.op(eng, lambda e: e.tensor_tensor(out=out, in0=in0, in1=in1, op=op), reads, writes)

        def ts(eng, out, in0, s1, op0, reads, writes, s2=None, op1=None):
            if op1 is None:
                T.op(eng, lambda e: e.tensor_scalar(out=out, in0=in0, scalar1=s1, scalar2=None, op0=op0),
                     reads, writes)
            else:
                T.op(eng, lambda e: e.tensor_scalar(out=out, in0=in0, scalar1=s1, scalar2=s2, op0=op0, op1=op1),
                     reads, writes)

        def stt(eng, out, in0, scalar, in1, op0, op1, reads, writes):
            T.op(eng, lambda e: e.scalar_tensor_tensor(out=out, in0=in0, scalar=scalar, in1=in1, op0=op0, op1=op1),
                 reads, writes)

        def cp(eng, out, in_, reads, writes):
            T.op(eng, lambda e: e.tensor_copy(out=out, in_=in_), reads, writes)

        def recip(out, in_, reads, writes):
            T.op("dve", lambda e: e.reciprocal(out=out, in_=in_), reads, writes)

        def memset(eng, ap, val, writes):
            T.op(eng, lambda e: e.memset(ap, val), (), writes)

        def dma(eng, sem, out, in_, reads, writes, mld=None):
            if mld is None:
                T.dma(eng, sem, lambda e: e.dma_start(out=out, in_=in_), reads, writes)
            else:
                T.dma(eng, sem, lambda e: e.dma_start(out=out, in_=in_, max_dma_last_dim=mld), reads, writes)

        def vcol(name, idx):
            o = VOFF[name] + idx
            return vecs[:, o:o + 1]

        def layer_tiles(l):
            out = []
            for n in range(NCH):
                out.append((wbin_d[l, n], win_d[l, n, :, 0:KC, :].rearrange("p k c -> p (k c)"), ("wb", l, "i", n)))
            for dc in range(16):
                out.append((wbout_d[l, dc], wout_d[l, dc, :, 0:KC, :].rearrange("p k c -> p (k c)"), ("wb", l, "o", dc)))
            for dc in range(16):
                out.append((wbpg_d[l, dc], wpg_d[l, dc, :, 0:KC, :].rearrange("p k c -> p (k c)"), ("wb", l, "g", dc)))
            return out
        LT = {l: layer_tiles(l) for l in layers}
        NT = NCH + 32
        wseq = []
        for s in range(nseq):
            for l in layers:
                for g in range(NG):
                    wseq.extend(LT[l])
        w_issued = [0]
        w_used = [0]
        cv_i = [0]
        NCV = 12

        def convert(l, i0, i1):
            for i in range(i0, min(i1, NT)):
                dst, src, key = LT[l][i]
                cs = cv_i[0] % NCV
                cv_i[0] += 1
                dma("pool", f"cv{cs}", dst, src, (), [key, ("cvsem", cs)], mld=512)

        def issue_w():
            i = w_issued[0]
            if i >= len(wseq):
                return
            sl = i % NWS
            src, _, key = wseq[i]
            dma("pool", f"w{sl}", wsl[sl][:].rearrange("p k c -> p (k c)"), src, [key], [("w", sl)])
            w_issued[0] += 1

        def next_w():
            i = w_used[0]
            while w_issued[0] < min(len(wseq), i + NWS):
                issue_w()
            w_used[0] += 1
            sl = i % NWS
            return wsl[sl], ("w", sl)

        dma("sp", "c0", identf[:], ident_d, (), ["identf"])
        dma("sp", "c1", vecs[:], vecs_d, (), ["vecs"])
        cp("dve", identb[:], identf[:], ["identf"], ["identb"])
        memset("dve", onesb[:], 1.0, ["onesb"])
        memset("dve", epsc[:], EPS, ["epsc"])
        so = VOFF["sink"]
        act(esink[:], vecs[:, so:so + NL * 8], AF.Exp, ["vecs"], ["esink"])
        m0, m0k, m0s = next_hs()
        m1, m1k, m1s = next_hs()
        dma("sp", m0s, m0[:, 0:384], mask_d[0], (), [m0k])
        dma("sp", m1s, m1[:, 0:384], mask_d[1], (), [m1k])
        for h in range(16):
            bt, btk, bts = hs[2 + h % 2], ("hs", 2 + h % 2), f"hs{2 + h % 2}"
            dma("sp", bts, bt[:, 0:384], bias_d[h], (), [btk])
            tt("dve", tf[h % 2][:, 0:384], bt[:, 0:384], m0[:, 0:384], ALU.mult,
               [btk, m0k], [("tf", h % 2)])
            tt("dve", biasT[:, h, :], tf[h % 2][:, 0:384], m1[:, 0:384], ALU.add,
               [("tf", h % 2), m1k], ["biasT"])
        T.barrier()

        def stats_load(s, tok0, n, kc):
            ht, hk, hsn = next_hs()
            dma("sp", hsn, ht[:, 0:n], hT_d[s, kc * 128:(kc + 1) * 128, tok0:tok0 + n],
                [("hT", s, kc)], [hk])
            return ht, hk

        def stats_compute(n, kc, slot):
            ht, hk = slot
            pieces = [(0, min(n, 512))] + ([(512, n)] if n > 512 else [])
            sl = kc % 2
            act(sqb[sl][:, 0:n], ht[:, 0:n], AF.Square, [hk], [("sqb", sl)])
            for pi, (a, b) in enumerate(pieces):
                mm(ST[pi][:, 0:b - a], onesb[:], sqb[sl][:, a:b], kc == 0, kc == KC - 1,
                   ["onesb", ("sqb", sl)], [("ps", "st", pi)], inc=True)

        def stats_step(s, tok0, n, kc):
            stats_compute(n, kc, stats_load(s, tok0, n, kc))

        def stats_finish(n, dst, dkey):
            pieces = [(0, min(n, 512))] + ([(512, n)] if n > 512 else [])
            for pi, (a, b) in enumerate(pieces):
                act(tmp640[:, a:b], ST[pi][:, 0:b - a], AF.Sqrt, [("ps", "st", pi)], ["tmp640"],
                    bias=epsc[:, 0:1], scale=1.0 / D)
            recip(dst[:, 0:n], tmp640[:, 0:n], ["tmp640"], [dkey])

        def rms_stats(s, tok0, n, scale_inv):
            for kc in range(KC):
                stats_step(s, tok0, n, kc)
            stats_finish(n, rstd, "rstd")

        hbf = hb[:].rearrange("p k t -> p (k t)").bitcast(F32)

        def stage0_real(s):
            for tg in range(S // 128):
                par = tg % 2
                xbuf = scrA[:, par * 2048:(par + 1) * 2048]
                stg = hbf[:, par * 2048:(par + 1) * 2048].rearrange("p (k t) -> p k t", k=KC)
                dma("sp", f"x{par}", xbuf, x_d[s, tg * 128:(tg + 1) * 128, :], (), [("xb", par)])
                for k4 in range(4):
                    bank, bkey = next_mm()
                    for kk in range(4):
                        kc = k4 * 4 + kk
                        tr(bank[:, kk * 128:(kk + 1) * 128], xbuf[:, kc * 128:(kc + 1) * 128],
                           [("xb", par)], [bkey], inc=(kk == 3))
                    eng = "dve" if k4 % 2 == 0 else "act"
                    src = bank[:].rearrange("p (k t) -> p k t", k=4)
                    if eng == "dve":
                        cp("dve", stg[:, k4 * 4:(k4 + 1) * 4, :], src, [bkey], [("stg", par)])
                    else:
                        act(stg[:, k4 * 4:(k4 + 1) * 4, :], src, AF.Copy, [bkey], [("stg", par)])
                dst = hT_d[s].rearrange("(k p) t -> p k t", p=128)[:, :, tg * 128:(tg + 1) * 128]
                dma("sp", f"stg{par}", dst, stg, [("stg", par)], [("hT", s, k) for k in range(KC)])

        def epilogue(s):
            fo = VOFF["final_g"]
            for tq in range(S // 256):
                tok0 = tq * 256
                rms_stats(s, tok0, 256, 1.0 / D)
                if os.environ.get("KSTOP", "") == "rms":
                    return
                ostv = scrA[:, 0:4096].rearrange("p (b d) -> p b d", b=2)
                for kc in range(KC):
                    sl = kc % 2
                    ht, hk, hsn = next_hs()
                    dma("sp", hsn, ht[:, 0:256], hT_d[s, kc * 128:(kc + 1) * 128, tok0:tok0 + 256],
                        [("hT", s, kc)], [hk])
                    stt("dve", tf[sl][:, 0:256], ht[:, 0:256], vecs[:, fo + kc:fo + kc + 1], rstd[:, 0:256],
                        ALU.mult, ALU.mult, [hk, "vecs", "rstd"], [("tf", sl)])
                    bank, bkey = next_mm()
                    for tb in range(2):
                        tr(bank[:, tb * 128:(tb + 1) * 128], tf[sl][:, tb * 128:(tb + 1) * 128],
                           [("tf", sl)], [bkey], inc=(tb == 1))
                    src = bank[:, 0:256].rearrange("p (b d) -> p b d", b=2)
                    if kc % 2 == 0:
                        act(ostv[:, :, kc * 128:(kc + 1) * 128], src, AF.Copy, [bkey], ["ost"])
                    else:
                        cp("dve", ostv[:, :, kc * 128:(kc + 1) * 128], src, [bkey], ["ost"])
                for b in range(2):
                    dma("sp", f"ost{b}", out_d[s, tok0 + b * 128:tok0 + (b + 1) * 128, :], ostv[:, b, :],
                        ["ost"], [("out", s)])

        pending_stats = [None]

        def layer_group(s, l, g, nxt=None):
            T0 = g * G
            Wn = min(640, S - T0)
            KST = os.environ.get("KSTOP", "")
            ebv = ebuf.rearrange("p (k t) -> p k t", k=16)
            cyv = cy.rearrange("p (k t) -> p k t", k=8)
            if pending_stats[0] == (s, l, g):
                rs, rsk = rsN, "rsN"
            else:
                rms_stats(s, T0, Wn, 1.0 / D)
                rs, rsk = rstd, "rstd"
            go = VOFF["norm_g"] + l * 16
            for kc in range(KC):
                sl = kc % 2
                ht, hk, hsn = next_hs()
                dma("sp", hsn, ht[:, 0:Wn], hT_d[s, kc * 128:(kc + 1) * 128, T0:T0 + Wn],
                    [("hT", s, kc)], [hk])
                stt("dve", hb[:, kc, 0:Wn], ht[:, 0:Wn], vecs[:, go + kc:go + kc + 1], rs[:, 0:Wn],
                    ALU.mult, ALU.mult, [hk, "vecs", rsk], [("hb", kc)])
            if KST == "hb":
                return
            akeys = [("a", c) for c in range(8)]
            if g == 0:
                memset("pool", abuf[:, :, 0:16], 0.0, akeys)
            else:
                T.op("pool", lambda e: e.tensor_copy(out=abuf[:, :, 0:144], in_=abuf[:, :, 512:656]), akeys, akeys)
            if g == NG - 1:
                memset("pool", abuf[:, :, 528:544], 0.0, akeys)
            if g == 0:
                lead = [(0, 512), (512, 640)]
            elif g == NG - 1:
                lead = [(128, 512)]
            else:
                lead = [(128, 640)]
            main = [(0, 512)]
            hbk = [("hb", kc) for kc in range(KC)]
            for n, (typ, ci) in enumerate(CH):
                if n >= int(os.environ.get("KNCH", 1000)):
                    return
                w, wkey = next_w()
                if os.environ.get("KAFTERW"):
                    return
                if typ == "v":
                    blocks = []
                    for (a, b) in lead:
                        blocks += list(range(a // 128, b // 128))
                    for b0 in range(0, len(blocks), 4):
                        bl = blocks[b0:b0 + 4]
                        bank, bkey = next_mm()
                        for bi, blk in enumerate(bl):
                            for kc in range(KC):
                                mm(bank[:, bi * 128:(bi + 1) * 128], hb[:, kc, blk * 128:(blk + 1) * 128], w[:, kc, :],
                                   kc == 0, kc == KC - 1, [("hb", kc), wkey], [bkey],
                                   inc=(kc == KC - 1 and bi == len(bl) - 1))
                        for bi, blk in enumerate(bl):
                            KB = (T0 + blk * 128) // 128
                            ks = KB % 6
                            cp("dve", vbuf[:, ks, ci * 128:(ci + 1) * 128], bank[:, bi * 128:(bi + 1) * 128],
                               [bkey], [("v", ks)])
                    continue
                pieces = lead if typ in ("k", "ag", "av") else main
                for (a, b) in pieces:
                    nt = b - a
                    bank, bkey = next_mm()
                    for kc in range(KC):
                        mm(bank[:, 0:nt], w[:, kc, :], hb[:, kc, a:b], kc == 0, kc == KC - 1,
                           [("hb", kc), wkey], [bkey], inc=(kc == KC - 1))
                    if os.environ.get("KNOEVAC"):
                        continue
                    if typ == "k":
                        for blk in range(a // 128, b // 128):
                            KB = (T0 + blk * 128) // 128
                            ks = KB % 6
                            o0 = blk * 128 - a
                            if True:
                                cp("dve", kbuf[:, ci, ks, :], bank[:, o0:o0 + 128], [bkey], [("k", ci, ks)])
                            else:
                                act(kbuf[:, ci, ks, :], bank[:, o0:o0 + 128], AF.Copy, [bkey], [("k", ci, ks)])
                    elif typ == "ag":
                        act(sgt[ci % 2][:, a:b], bank[:, 0:nt], AF.Sigmoid, [bkey], [("sgt", ci % 2)])
                    elif typ == "av":
                        tt("dve", abuf[:, ci, 16 + a:16 + b], bank[:, 0:nt], sgt[ci % 2][:, a:b], ALU.mult,
                           [bkey, ("sgt", ci % 2)], [("a", ci)])
                    elif typ == "az":
                        act(ybuf[:, ci, :], bank[:, 0:nt], AF.Silu, [bkey], [("y", ci)])
                    elif typ == "bz":
                        act(ybuf[:, 8 + ci, :], bank[:, 0:nt], AF.Silu, [bkey], [("y", 8 + ci)])
                    elif typ == "q":
                        ts("dve", qbuf[:, ci, :], bank[:, 0:nt], 0.125, ALU.mult, [bkey], [("q", ci)])
            if KST == "inproj":
                return
            cwo = VOFF["conv_w"] + l * 8 * 31
            for c in range(8):
                ds = c % 2
                idb = identb[:]
                cwv = vecs[:, cwo + c * 31:cwo + c * 31 + 31]
                tt(os.environ.get("KDGENG", "dve"), dg[ds][:], bass.AP(idb.tensor, idb.offset, [idb.ap[0], [0, 31], idb.ap[1]]),
                   bass.AP(cwv.tensor, cwv.offset, [cwv.ap[0], cwv.ap[1], [0, 128]]), ALU.mult,
                   ["identb", "vecs"], [("dg", ds)])
                for k in range(31):
                    mm(SC[ds][:, :], dg[ds][:, k, :], abuf[:, c, k + 1:k + 1 + 512], k == 0, k == 30,
                       [("dg", ds), ("a", c)], [("ps", "sc", ds)], inc=(k == 30))
                act(cyv[:, c, :], SC[ds][:, :], AF.Identity, [("ps", "sc", ds)], [("cy", c)],
                    bias=vcol("conv_b", l * 8 + c))
                tt("dve", ysq[ds][:], cyv[:, c, :], cyv[:, c, :], ALU.mult, [("cy", c)], [("ysq", ds)])
                for cp_ in ([c - 1] if c > 0 else []) + ([7] if c == 7 else []):
                    dp = cp_ % 2
                    mm(ST[0][:, :], onesb[:], cyv[:, cp_, :], cp_ == 0, cp_ == 7, ["onesb", ("cy", cp_)], [("ps", "st", 0)], inc=True)
                    mm(ST[1][:, :], onesb[:], ysq[dp][:], cp_ == 0, cp_ == 7, ["onesb", ("ysq", dp)], [("ps", "st", 1)], inc=True)
            ts("dve", meant[:], ST[0][:, :], 1.0 / 1024, ALU.mult, [("ps", "st", 0)], ["meant"])
            tt("dve", tmp640[:, 0:512], meant[:], meant[:], ALU.mult, ["meant"], ["tmp640"])
            stt("dve", tmp640[:, 0:512], ST[1][:, :], 1.0 / 1024, tmp640[:, 0:512], ALU.mult, ALU.subtract,
                [("ps", "st", 1), "tmp640"], ["tmp640"])
            act(tmp640[:, 0:512], tmp640[:, 0:512], AF.Sqrt, ["tmp640"], ["tmp640"], bias=epsc[:, 0:1], scale=1.0)
            recip(lnA[:], tmp640[:, 0:512], ["tmp640"], ["lnA"])
            stt("dve", lnB[:], meant[:], -1.0, lnA[:], ALU.mult, ALU.mult, ["meant", "lnA"], ["lnB"])
            for c in range(8):
                sl = c % 2
                tt("dve", tf[sl][:], cyv[:, c, :], lnA[:], ALU.mult, [("cy", c), "lnA"], [("tf", sl)])
                tt("dve", tf[sl][:], tf[sl][:], lnB[:], ALU.add, [("tf", sl), "lnB"], [("tf", sl)])
                act(tb16[sl][:], tf[sl][:], AF.Silu, [("tf", sl), "vecs"], [("tb16", sl)],
                    bias=vcol("cln_b", l * 8 + c), scale=vcol("cln_g", l * 8 + c))
                tt("pool", ybuf[:, c, :], tb16[sl][:], ybuf[:, c, :], ALU.mult, [("tb16", sl), ("y", c)], [("y", c)])
            if KST == "conv":
                return
            it = 0
            for qb in range(4):
                Q = 4 * g + qb
                kbs = [kb for kb in range(3) if 0 <= Q + kb - 1 < S // 128]
                for j in range(4):
                    pts = []
                    for kb in kbs:
                        ks = (Q + kb - 1) % 6
                        scb = it % 2
                        pti = it % 6
                        it += 1
                        sck = ("ps", "sc", scb)
                        for hh in range(4):
                            c = 2 * j + hh // 2
                            hf = hh % 2
                            mm(SC[scb][:, hh * 128:(hh + 1) * 128], identb[:],
                               biasT[:, 4 * j + hh, kb * 128:(kb + 1) * 128],
                               True, False, ["identb", "biasT"], [sck], inc=False)
                            mm(SC[scb][:, hh * 128:(hh + 1) * 128], kbuf[hf * 64:(hf + 1) * 64, j, ks, :],
                               qbuf[hf * 64:(hf + 1) * 64, c, qb * 128:(qb + 1) * 128], False, True,
                               [("k", j, ks), ("q", c)], [sck], inc=(hh == 3))
                        act(PT[pti][:], SC[scb][:, :], AF.Exp, [sck], [("pt", pti)])
                        pts.append((pti, ks))
                    pvk = ("ps", "pv")
                    if os.environ.get("KATT") == "s":
                        continue
                    for cc in range(2):
                        for hf in range(2):
                            hh = 2 * cc + hf
                            for i, (pti, ks) in enumerate(pts):
                                mm(PV[hf * 64:(hf + 1) * 64, cc * 128:(cc + 1) * 128], vbuf[:, ks, j * 64:(j + 1) * 64],
                                   PT[pti][:, hh * 128:(hh + 1) * 128], i == 0, i == len(pts) - 1,
                                   [("v", ks), ("pt", pti)], [pvk], inc=False, tp=((0, 64) if hf else None))
                            for i, (pti, ks) in enumerate(pts):
                                mm(PV[hf * 64:(hf + 1) * 64, 256 + cc * 128:256 + (cc + 1) * 128], onesb[:, 0:64],
                                   PT[pti][:, hh * 128:(hh + 1) * 128], i == 0, i == len(pts) - 1,
                                   ["onesb", ("pt", pti)], [pvk],
                                   inc=(cc == 1 and hf == 1 and i == len(pts) - 1), tp=((0, 64) if hf else None))
                    dsl = (qb * 4 + j) % 2
                    for cc in range(2):
                        ts("dve", dens[dsl][:, cc * 128:(cc + 1) * 128], PV[:, 256 + cc * 128:256 + (cc + 1) * 128],
                           esink[:, l * 8 + j * 2 + cc:l * 8 + j * 2 + cc + 1], ALU.add, [pvk, "esink"], [("dens", dsl)])
                    recip(dens[dsl][:], dens[dsl][:], [("dens", dsl)], [("dens", dsl)])
                    yv = ybuf[:, 8 + 2 * j:8 + 2 * j + 2, qb * 128:(qb + 1) * 128]
                    yk = [("y", 8 + 2 * j), ("y", 8 + 2 * j + 1)]
                    tt("dve", rz[dsl][:].rearrange("p (c q) -> p c q", c=2), dens[dsl][:].rearrange("p (c q) -> p c q", c=2),
                       yv, ALU.mult, [("dens", dsl)] + yk, [("rz", dsl)])
                    tt("dve", yv, PV[:, 0:256].rearrange("p (c q) -> p c q", c=2),
                       rz[dsl][:].rearrange("p (c q) -> p c q", c=2), ALU.mult, [pvk, ("rz", dsl)], yk)
            if KST == "attn":
                return
            LA = 2
            pre = {}

            def pf(dc):
                ht, hk, hsn = next_hs()
                dma("sp", hsn, ht[:, 0:G], hT_d[s, dc * 128:(dc + 1) * 128, T0:T0 + G], [("hT", s, dc)], [hk])
                pre[dc] = (ht, hk, hsn)
            for dc in range(LA):
                pf(dc)
            spre = {}
            if nxt is not None:
                T0n = nxt[2] * G
                Wnn = min(640, S - T0n)
                for dc in range(LA):
                    spre[dc] = stats_load(nxt[0], T0n, Wnn, dc)
            for dc in range(16):
                w, wkey = next_w()
                bank, bkey = next_mm()
                for kc in range(KC):
                    mm(bank[:, :], w[:, kc, :], ybuf[:, kc, :], kc == 0, kc == KC - 1, [("y", kc), wkey], [bkey],
                       inc=(kc == KC - 1))
                if dc + LA < 16:
                    pf(dc + LA)
                if nxt is not None:
                    if dc + LA < 16:
                        spre[dc + LA] = stats_load(nxt[0], T0n, Wnn, dc + LA)
                    stats_compute(Wnn, dc, spre.pop(dc))
                ht, hk, hsn = pre.pop(dc)
                tt("dve", ht[:, 0:G], ht[:, 0:G], bank[:, :], ALU.add, [hk, bkey], [hk])
                act(hb[:, dc, 0:G], ht[:, 0:G], AF.Copy, [hk], [("hb", dc)])
                dma("sp", hsn, hT_d[s, dc * 128:(dc + 1) * 128, T0:T0 + G], ht[:, 0:G], [hk], [("hT", s, dc)])
            if nxt is not None:
                stats_finish(Wnn, rsN, "rsN")
                pending_stats[0] = nxt
            if KST == "outproj":
                return
            dma("sp", "ptile", ptile[:], p_d[l, s, T0:T0 + G, :].rearrange("(b p) f -> p b f", p=128), (), ["ptile"])
            for pc in range(2):
                for tb in range(4):
                    tr(ST[pc][:, tb * 128:(tb + 1) * 128], ptile[:, tb, pc * 128:(pc + 1) * 128], ["ptile"],
                       [("ps", "st", pc)], inc=(tb == 3))
                cp("dve", pT[:, pc, :], ST[pc][:, :], [("ps", "st", pc)], [("pT", pc)])
            for dc in range(16):
                bank, bkey = next_mm()
                for pc in range(2):
                    mm(bank[:, :], wpe[:, pc, dc * 128:(dc + 1) * 128], pT[:, pc, :], pc == 0, pc == 1,
                       ["wpe", ("pT", pc)], [bkey], inc=(pc == 1))
                sq4 = [(ysq[0], ("ysq", 0)), (ysq[1], ("ysq", 1)), (tb16[0], ("tb16", 0)), (tb16[1], ("tb16", 1))]
                sqt, sqk = sq4[dc % 4]
                act(ebv[:, dc, :], bank[:, :], AF.Copy, [bkey], [("e", dc)])
                tt("dve", sqt[:], ebv[:, dc, :], ebv[:, dc, :], ALU.mult, [("e", dc)], [sqk])
                for dp in ([dc - 3] if dc >= 3 else []) + ([13, 14, 15] if dc == 15 else []):
                    sqt2, sqk2 = sq4[dp % 4]
                    mm(ST[0][:, :], onesb[:], sqt2[:], dp == 0, dp == 15, ["onesb", sqk2], [("ps", "st", 0)], inc=True)
            act(tmp640[:, 0:512], ST[0][:, :], AF.Sqrt, [("ps", "st", 0)], ["tmp640"], bias=epsc[:, 0:1], scale=1.0 / D)
            recip(rstd[:, 0:512], tmp640[:, 0:512], ["tmp640"], ["rstd"])
            pgo = VOFF["pe_g"] + l * 16
            pre2 = {}

            def pf2(dc):
                ht, hk, hsn = next_hs()
                dma("sp", hsn, ht[:, 0:G], hT_d[s, dc * 128:(dc + 1) * 128, T0:T0 + G], [("hT", s, dc)], [hk])
                pre2[dc] = (ht, hk, hsn)
            for dc in range(LA):
                pf2(dc)
            for dc in range(16):
                w, wkey = next_w()
                bank, bkey = next_mm()
                for kc in range(KC):
                    mm(bank[:, :], w[:, kc, :], hb[:, kc, 0:G], kc == 0, kc == KC - 1, [("hb", kc), wkey], [bkey],
                       inc=(kc == KC - 1))
                if dc + LA < 16:
                    pf2(dc + LA)
                sl = dc % 2
                act(sgb[sl][:], bank[:, :], AF.Sigmoid, [bkey], [("sg", sl)])
                tt("dve", tf[sl][:], ebv[:, dc, :], rstd[:, 0:512], ALU.mult, [("e", dc), "rstd"], [("tf", sl)])
                stt("dve", tf[sl][:], tf[sl][:], vecs[:, pgo + dc:pgo + dc + 1], sgb[sl][:], ALU.mult, ALU.mult,
                    [("tf", sl), "vecs", ("sg", sl)], [("tf", sl)])
                ht, hk, hsn = pre2.pop(dc)
                tt("dve", ht[:, 0:G], ht[:, 0:G], tf[sl][:], ALU.add, [hk, ("tf", sl)], [hk])
                dma("sp", hsn, hT_d[s, dc * 128:(dc + 1) * 128, T0:T0 + G], ht[:, 0:G], [hk], [("hT", s, dc)])

        KSTOP = os.environ.get("KSTOP", "")
        convert(layers[0], 0, NT) if len(layers) else None
        for s in range(nseq):
            if KSTOP == "setup":
                break
            if do_stage0:
                stage0_real(s)
                T.barrier()
            if KSTOP == "stage0":
                break
            for l in layers:
                for pc in range(2):
                    if os.environ.get("KNOWPE"):
                        continue
                    dma("pool", f"wpe{pc}", wpe[:, pc, :], wpe_d[l, :, pc, :], (), ["wpe"], mld=int(os.environ.get("KMLD", 512)))
                for g in range(int(os.environ.get("KGROUPS", NG))):
                    li = layers.index(l)
                    if s == 0 and li + 1 < len(layers):
                        convert(layers[li + 1], g * 20, (g + 1) * 20)
                    ng = int(os.environ.get("KGROUPS", NG))
                    if g + 1 < ng:
                        nxt = (s, l, g + 1)
                    elif li + 1 < len(layers) and ng == NG:
                        nxt = (s, layers[li + 1], 0)
                    else:
                        nxt = None
                    layer_group(s, l, g, nxt)
            T.barrier()
            if do_final:
                epilogue(s)
                T.barrier()
        T.barrier()

        names = T.sem_names()
        sems = {n: es.enter_context(nc.semaphore("s_" + n)) for n in names}
        with nc.Block() as block:
            def emit(kind):
                def run(e):
                    for waits, fn, inc in T.ops[kind]:
                        for sname, v in waits:
                            e.wait_ge(sems[sname], v)
                        ins = fn(e)
                        if inc is not None:
                            ins.then_inc(sems[inc[0]], inc[1])
                    for sname, v in T.pending[kind].items():
                        e.wait_ge(sems[sname], v)
                return run
            block.tensor(emit("pe"))
            block.scalar(emit("act"))
            block.vector(emit("dve"))
            block.gpsimd(emit("pool"))
            block.sync(emit("sp"))
    return nc


def prep_shared(norm_g, w_in, conv_w, conv_b, cln_g, cln_b, sink, rel_bias, w_out, w_pe, pe_g, w_pg, final_g):
    f = lambda a: np.ascontiguousarray(np.asarray(a, dtype=np.float32))
    w_in, w_out, w_pe, w_pg = f(w_in), f(w_out), f(w_pe), f(w_pg)
    idx = _chunk_cols()
    wi = w_in[:, :, idx.reshape(-1)].reshape(NL, KC, 128, NCH, 128)
    def padk(a):
        o = np.zeros(a.shape[:3] + (KC + 1, 128), np.float32)
        o[:, :, :, :KC, :] = a
        return o
    wi = padk(wi.transpose(0, 3, 2, 1, 4))
    wo = padk(w_out.reshape(NL, KC, 128, 16, 128).transpose(0, 3, 2, 1, 4))
    wg = padk(w_pg.reshape(NL, KC, 128, 16, 128).transpose(0, 3, 2, 1, 4))
    wp = np.ascontiguousarray(w_pe.reshape(NL, 2, 128, D).transpose(0, 2, 1, 3))
    vecs = _pack_vecs(f(norm_g), f(pe_g), f(final_g), f(conv_b), f(cln_g), f(cln_b), f(conv_w), f(sink))
    buckets, band = _t5_band_buckets()
    gathered = f(rel_bias)[buckets]
    biasg = np.ascontiguousarray(gathered.reshape(128, 3, 128, 16).transpose(3, 2, 1, 0)).reshape(16, 128, 384)
    m01 = band.astype(np.float32).reshape(128, 3, 128).transpose(2, 1, 0).reshape(128, 384)
    masks = np.ascontiguousarray(np.stack([m01, (m01 - 1.0) * 30000.0]).astype(np.float32))
    return {"w_in": wi, "w_out": wo, "w_pg": wg, "w_pe": wp, "vecs": vecs, "biasg": biasg, "masks": masks,
            "ident": np.eye(128, dtype=np.float32)}


def kernel(x, p, norm_g, w_in, conv_w, conv_b, cln_g, cln_b, sink, rel_bias, w_out, w_pe, pe_g, w_pg, final_g):
    x = np.asarray(x, dtype=np.float32)
    p = np.asarray(p, dtype=np.float32)
    shared = prep_shared(norm_g, w_in, conv_w, conv_b, cln_g, cln_b, sink, rel_bias, w_out, w_pe, pe_g, w_pg, final_g)
    nc = build_program()
    in_maps = []
    for c in range(NCORES):
        m = dict(shared)
        m["x"] = np.ascontiguousarray(x[c * SEQ_PER_CORE:(c + 1) * SEQ_PER_CORE])
        m["p"] = np.ascontiguousarray(p[:, c * SEQ_PER_CORE:(c + 1) * SEQ_PER_CORE])
        in_maps.append(m)
    res = run_bass_kernel_spmd(nc, in_maps, core_ids=list(range(NCORES)))
    out = np.concatenate([r["out"] for r in res.results], axis=0)
    return out.astype(np.float32)
```

```python
import os
import numpy as np
import concourse.bass as bass
import concourse.mybir as mybir
from concourse.bass_utils import run_bass_kernel_spmd

F32 = mybir.dt.float32
BF16 = mybir.dt.bfloat16
AF = mybir.ActivationFunctionType
ALU = mybir.AluOpType

D = 2048
S = 2048
NL = 4
G = 512
NG = S // G
KC = 16
EPS = 1e-6
NCORES = 8
SEQ_PER_CORE = 2
NWS = int(os.environ.get("KNWS", 3))

CH = []
for _j in range(4):
    CH.append(("k", _j))
for _vc in range(2):
    CH.append(("v", _vc))
for _c in range(8):
    CH.append(("ag", _c))
    CH.append(("av", _c))
for _c in range(8):
    CH.append(("az", _c))
for _c in range(8):
    CH.append(("q", _c))
for _c in range(8):
    CH.append(("bz", _c))
NCH = len(CH)


def _chunk_cols():
    AV0, AG0, AZ0, Q0, K0, V0, BZ0 = 0, 1024, 2048, 3072, 4096, 4352, 4608
    idx = np.zeros((NCH, 128), np.int64)
    r = np.arange(128)
    for n, (t, i) in enumerate(CH):
        if t == "k":
            idx[n] = K0 + i * 64 + (r % 64)
        elif t == "v":
            idx[n] = V0 + i * 128 + r
        elif t == "ag":
            idx[n] = AG0 + i * 128 + r
        elif t == "av":
            idx[n] = AV0 + i * 128 + r
        elif t == "az":
            idx[n] = AZ0 + i * 128 + r
        elif t == "q":
            idx[n] = Q0 + i * 128 + r
        elif t == "bz":
            idx[n] = BZ0 + i * 128 + r
    return idx


VOFF = {}
_o = 0
for _name, _n in (("norm_g", NL * 16), ("pe_g", NL * 16), ("final_g", 16), ("conv_b", NL * 8),
                  ("cln_g", NL * 8), ("cln_b", NL * 8), ("conv_w", NL * 8 * 31), ("sink", NL * 8)):
    VOFF[_name] = _o
    _o += _n
NV = _o


def _pack_vecs(norm_g, pe_g, final_g, conv_b, cln_g, cln_b, conv_w, sink):
    v = np.zeros((128, NV), np.float32)
    def put(name, arr):
        v[:, VOFF[name]:VOFF[name] + arr.shape[1]] = arr
    put("norm_g", norm_g.reshape(NL, 16, 128).transpose(2, 0, 1).reshape(128, -1))
    put("pe_g", pe_g.reshape(NL, 16, 128).transpose(2, 0, 1).reshape(128, -1))
    put("final_g", final_g.reshape(16, 128).T)
    put("conv_b", conv_b.reshape(NL, 8, 128).transpose(2, 0, 1).reshape(128, -1))
    put("cln_g", cln_g.reshape(NL, 8, 128).transpose(2, 0, 1).reshape(128, -1))
    put("cln_b", cln_b.reshape(NL, 8, 128).transpose(2, 0, 1).reshape(128, -1))
    put("conv_w", conv_w.reshape(NL, 31, 8, 128).transpose(3, 0, 2, 1).reshape(128, -1))
    sk = np.zeros((128, NL, 4, 2), np.float32)
    for j in range(4):
        for cc in range(2):
            sk[:64, :, j, cc] = sink[:, 4 * j + 2 * cc][None, :]
            sk[64:, :, j, cc] = sink[:, 4 * j + 2 * cc + 1][None, :]
    put("sink", sk.reshape(128, -1))
    return v


def _t5_band_buckets():
    BLK, NUM_BUCKETS, MAX_DISTANCE, WINDOW = 128, 32, 128, 128
    q_off = np.arange(BLK)[:, None]
    k_off = np.arange(3 * BLK)[None, :] - BLK
    rel = k_off - q_off
    half = NUM_BUCKETS // 2
    ret = (rel > 0).astype(np.int32) * half
    n = np.abs(rel)
    max_exact = half // 2
    large = max_exact + (np.log(np.maximum(n, 1) / max_exact)
                         / np.log(MAX_DISTANCE / max_exact) * (half - max_exact)).astype(np.int32)
    large = np.minimum(large, half - 1)
    ret = ret + np.where(n < max_exact, n, large)
    return ret.astype(np.int32), (n <= WINDOW)


class Tracker:
    ENG = ("pe", "act", "dve", "pool", "sp")

    def __init__(self):
        self.ops = {e: [] for e in self.ENG}
        self.cnt = {e: 0 for e in self.ENG}
        self.dcnt = {}
        self.last_w = {}
        self.rd = {}
        self.seen = {e: {} for e in self.ENG}
        self.pending = {e: {} for e in self.ENG}

    def _deps(self, eng, reads, writes):
        deps = dict(self.pending[eng])
        self.pending[eng] = {}

        def add(ev):
            if ev is None:
                return
            s, v = ev
            if deps.get(s, 0) < v:
                deps[s] = v
        for k in reads:
            add(self.last_w.get(k))
        for k in writes:
            add(self.last_w.get(k))
            for ev in self.rd.get(k, ()):
                add(ev)
        out = []
        seen = self.seen[eng]
        for s, v in deps.items():
            if s == "pe" and eng == "pe":
                continue
            if seen.get(s, 0) >= v:
                continue
            seen[s] = v
            out.append((s, v))
        return out

    def _record(self, ev, reads, writes):
        for k in writes:
            self.last_w[k] = ev
            self.rd[k] = []
        for k in reads:
            self.rd.setdefault(k, []).append(ev)

    def op(self, eng, fn, reads=(), writes=(), inc=True):
        waits = self._deps(eng, reads, writes)
        if inc:
            self.cnt[eng] += 1
            ev = (eng, self.cnt[eng])
        else:
            ev = (eng, self.cnt[eng] + 1)
        self._record(ev, reads, writes)
        self.ops[eng].append((waits, fn, (eng, 1) if inc else None))

    def dma(self, eng, sem, fn, reads=(), writes=()):
        waits = self._deps(eng, reads, writes)
        self.dcnt[sem] = self.dcnt.get(sem, 0) + 16
        ev = (sem, self.dcnt[sem])
        self._record(ev, reads, writes)
        self.ops[eng].append((waits, fn, (sem, 16)))

    def barrier(self):
        allv = dict(self.cnt)
        allv.update(self.dcnt)
        for e in self.ENG:
            for s, v in allv.items():
                if v > 0 and self.pending[e].get(s, 0) < v:
                    self.pending[e][s] = v

    def sem_names(self):
        return list(self.ENG) + sorted(self.dcnt.keys())


def build_program(nseq=SEQ_PER_CORE, layers=(0, 1, 2, 3), do_stage0=True, do_final=True, debug_h=False):
    nc = bass.Bass("TRN2", target_bir_lowering=False)
    T = Tracker()

    x_d = nc.dram_tensor("x", [nseq, S, D], F32, kind="ExternalInput").ap()
    p_d = nc.dram_tensor("p", [NL, nseq, S, 256], F32, kind="ExternalInput").ap()
    win_d = nc.dram_tensor("w_in", [NL, NCH, 128, KC + 1, 128], F32, kind="ExternalInput").ap()
    wout_d = nc.dram_tensor("w_out", [NL, 16, 128, KC + 1, 128], F32, kind="ExternalInput").ap()
    wpg_d = nc.dram_tensor("w_pg", [NL, 16, 128, KC + 1, 128], F32, kind="ExternalInput").ap()
    wpe_d = nc.dram_tensor("w_pe", [NL, 128, 2, D], F32, kind="ExternalInput").ap()
    vecs_d = nc.dram_tensor("vecs", [128, NV], F32, kind="ExternalInput").ap()
    bias_d = nc.dram_tensor("biasg", [16, 128, 384], F32, kind="ExternalInput").ap()
    mask_d = nc.dram_tensor("masks", [2, 128, 384], F32, kind="ExternalInput").ap()
    ident_d = nc.dram_tensor("ident", [128, 128], F32, kind="ExternalInput").ap()
    out_d = nc.dram_tensor("out", [nseq, S, D], F32, kind="ExternalOutput").ap()
    if debug_h:
        hT_t = nc.dram_tensor("hT", [nseq, D, S], F32, kind="ExternalOutput")
    else:
        hT_t = nc.dram_tensor("hT", [nseq, D, S], F32)
    hT_d = hT_t.ap()
    wbin_d = nc.dram_tensor("wb_in", [NL, NCH, 128, KC * 128], BF16).ap()
    wbout_d = nc.dram_tensor("wb_out", [NL, 16, 128, KC * 128], BF16).ap()
    wbpg_d = nc.dram_tensor("wb_pg", [NL, 16, 128, KC * 128], BF16).ap()

    from contextlib import ExitStack
    es = ExitStack()

    def sb(name, shape, dt):
        return es.enter_context(nc.sbuf_tensor(name, shape, dt))

    def ps(name):
        return es.enter_context(nc.psum_tensor(name, [128, 512], F32))

    with es:
        identf = sb("identf", [128, 128], F32)
        identb = sb("identb", [128, 128], BF16)
        onesb = sb("onesb", [128, 128], BF16)
        vecs = sb("vecs_sb", [128, NV], F32)
        esink = sb("esink", [128, NL * 8], F32)
        epsc = sb("epsc", [128, 1], F32)
        biasT = sb("biasT", [128, 16, 384], BF16)
        NHS = int(os.environ.get("KNHS", 8))
        hs = [sb(f"hs{i}", [128, 640], F32) for i in range(NHS)]
        hs_i = [0]

        def next_hs():
            i = hs_i[0] % NHS
            hs_i[0] += 1
            return hs[i], ("hs", i), f"hs{i}"
        sqb = [sb(f"sqb{i}", [128, 640], BF16) for i in range(2)]
        hb = sb("hb", [128, KC, 640], BF16)
        rstd = sb("rstd", [128, 640], F32)
        rsN = sb("rsN", [128, 640], F32)
        tmp640 = sb("tmp640", [128, 640], F32)
        wsl = [sb(f"w{i}", [128, KC, 128], BF16) for i in range(NWS)]
        wpe = sb("wpe", [128, 2, D], BF16)
        abuf = sb("abuf", [128, 8, 656], BF16)
        kbuf = sb("kbuf", [128, 4, 6, 128], BF16)
        vbuf = sb("vbuf", [128, 6, 256], BF16)
        qbuf = sb("qbuf", [128, 8, 512], BF16)
        ybuf = sb("ybuf", [128, 16, 512], BF16)
        dg = [sb(f"dg{i}", [128, 31, 128], BF16) for i in range(2)]
        scrA = sb("scrA", [128, 6144], F32)
        ysq = [sb(f"ysq{i}", [128, 512], BF16) for i in range(2)]
        meant = sb("meant", [128, 512], F32)
        lnA = sb("lnA", [128, 512], F32)
        lnB = sb("lnB", [128, 512], F32)
        tf = [sb(f"tf{i}", [128, 512], F32) for i in range(2)]
        tb16 = [sb(f"tb16_{i}", [128, 512], BF16) for i in range(2)]
        PT = [sb(f"pt{i}", [128, 512], BF16) for i in range(6)]
        dens = [sb(f"dens{i}", [128, 256], F32) for i in range(2)]
        rz = [sb(f"rz{i}", [128, 256], F32) for i in range(2)]
        ptile = sb("ptile", [128, 4, 256], F32)
        pT = sb("pT", [128, 2, 512], BF16)
        sgb = [sb(f"sg{i}", [128, 512], F32) for i in range(2)]
        sgt = [sb(f"sgt{i}", [128, 640], BF16) for i in range(2)]

        ebuf = scrA[:, 0:4096].bitcast(BF16)
        cy = scrA[:, 4096:6144].bitcast(BF16)
        xl = scrA[:, 0:3072]
        hst = scrA[:, 3072:6144]
        ost = scrA

        if os.environ.get("KVERBOSE"):
            print("SBUF_REMAINING", nc.sbuf_bytes_remaining)
        MM = [ps(f"mm{i}") for i in range(3)]
        ST = [ps(f"st{i}") for i in range(2)]
        SC = [ps(f"sc{i}") for i in range(2)]
        PV = ps("pv")

        mm_i = [0]

        def next_mm():
            i = mm_i[0] % 3
            mm_i[0] += 1
            return MM[i], ("ps", "mm", i)

        eng_of = {"pe": "tensor", "act": "scalar", "dve": "vector", "pool": "gpsimd", "sp": "sync"}

        def mm(out, lhsT, rhs, start, stop, reads, writes, inc, tp=None):
            if tp is None:
                T.op("pe", lambda e: e.matmul(out, lhsT, rhs, start=start, stop=stop), reads, writes, inc)
            else:
                T.op("pe", lambda e: e.matmul(out, lhsT, rhs, start=start, stop=stop, tile_position=tp),
                     reads, writes, inc)

        def tr(out, in_, reads, writes, inc):
            T.op("pe", lambda e: e.transpose(out, in_, identf[:]), list(reads) + ["identf"], writes, inc)

        def act(out, in_, func, reads, writes, bias=None, scale=None):
            kw = {}
            if bias is not None:
                kw["bias"] = bias
            if scale is not None:
                kw["scale"] = scale
            T.op("act", lambda e: e.activation(out=out, in_=in_, func=func, **kw), reads, writes)

        def tt(eng, out, in0, in1, op, reads, writes):
            T.op(eng, lambda e: e.tensor_tensor(out=out, in0=in0, in1=in1, op=op), reads, writes)

        def ts(eng, out, in0, s1, op0, reads, writes, s2=None, op1=None):
            if op1 is None:
                T.op(eng, lambda e: e.tensor_scalar(out=out, in0=in0, scalar1=s1, scalar2=None, op0=op0),
                     reads, writes)
            else:
                T.op(eng, lambda e: e.tensor_scalar(out=out, in0=in0, scalar1=s1, scalar2=s2, op0=op0, op1=op1),
                     reads, writes)

        def stt(eng, out, in0, scalar, in1, op0, op1, reads, writes):
            T.op(eng, lambda e: e.scalar_tensor_tensor(out=out, in0=in0, scalar=scalar, in1=in1, op0=op0, op1=op1),
                 reads, writes)

        def cp(eng, out, in_, reads, writes):
            T.op(eng, lambda e: e.tensor_copy(out=out, in_=in_), reads, writes)

        def recip(out, in_, reads, writes):
            T.op("dve", lambda e: e.reciprocal(out=out, in_=in_), reads, writes)

        def memset(eng, ap, val, writes):
            T.op(eng, lambda e: e.memset(ap, val), (), writes)

        def dma(eng, sem, out, in_, reads, writes, mld=None):
            if mld is None:
                T.dma(eng, sem, lambda e: e.dma_start(out=out, in_=in_), reads, writes)
            else:
                T.dma(eng, sem, lambda e: e.dma_start(out=out, in_=in_, max_dma_last_dim=mld), reads, writes)

        def vcol(name, idx):
            o = VOFF[name] + idx
            return vecs[:, o:o + 1]

        def layer_tiles(l):
            out = []
            for n in range(NCH):
                out.append((wbin_d[l, n], win_d[l, n, :, 0:KC, :].rearrange("p k c -> p (k c)"), ("wb", l, "i", n)))
            for dc in range(16):
                out.append((wbout_d[l, dc], wout_d[l, dc, :, 0:KC, :].rearrange("p k c -> p (k c)"), ("wb", l, "o", dc)))
            for dc in range(16):
                out.append((wbpg_d[l, dc], wpg_d[l, dc, :, 0:KC, :].rearrange("p k c -> p (k c)"), ("wb", l, "g", dc)))
            return out
        LT = {l: layer_tiles(l) for l in layers}
        NT = NCH + 32
        wseq = []
        for s in range(nseq):
            for l in layers:
                for g in range(NG):
                    wseq.extend(LT[l])
        w_issued = [0]
        w_used = [0]
        cv_i = [0]
        NCV = 12
        cv_pending = []

        def convert(l, i0, i1):
            for i in range(i0, min(i1, NT)):
                dst, src, key = LT[l][i]
                cs = cv_i[0] % NCV
                cv_i[0] += 1
                dma("pool", f"cv{cs}", dst, src, (), [key, ("cvsem", cs)], mld=512)

        def issue_w():
            i = w_issued[0]
            if i >= len(wseq):
                return
            sl = i % NWS
            src, _, key = wseq[i]
            dma("pool", f"w{sl}", wsl[sl][:].rearrange("p k c -> p (k c)"), src, [key], [("w", sl)])
            w_issued[0] += 1
            if cv_pending and w_issued[0] % 3 == 0:
                l2, i2 = cv_pending.pop(0)
                convert(l2, i2, i2 + 1)

        def next_w():
            i = w_used[0]
            while w_issued[0] < min(len(wseq), i + NWS):
                issue_w()
            w_used[0] += 1
            sl = i % NWS
            return wsl[sl], ("w", sl)

        dma("sp", "c0", identf[:], ident_d, (), ["identf"])
        dma("sp", "c1", vecs[:], vecs_d, (), ["vecs"])
        cp("dve", identb[:], identf[:], ["identf"], ["identb"])
        memset("dve", onesb[:], 1.0, ["onesb"])
        memset("dve", epsc[:], EPS, ["epsc"])
        so = VOFF["sink"]
        act(esink[:], vecs[:, so:so + NL * 8], AF.Exp, ["vecs"], ["esink"])
        m0, m0k, m0s = next_hs()
        m1, m1k, m1s = next_hs()
        dma("sp", m0s, m0[:, 0:384], mask_d[0], (), [m0k])
        dma("sp", m1s, m1[:, 0:384], mask_d[1], (), [m1k])
        for h in range(16):
            bt, btk, bts = hs[2 + h % 2], ("hs", 2 + h % 2), f"hs{2 + h % 2}"
            dma("sp", bts, bt[:, 0:384], bias_d[h], (), [btk])
            tt("dve", tf[h % 2][:, 0:384], bt[:, 0:384], m0[:, 0:384], ALU.mult,
               [btk, m0k], [("tf", h % 2)])
            tt("dve", biasT[:, h, :], tf[h % 2][:, 0:384], m1[:, 0:384], ALU.add,
               [("tf", h % 2), m1k], ["biasT"])
        T.barrier()

        def stats_load(s, tok0, n, kc):
            ht, hk, hsn = next_hs()
            dma("sp", hsn, ht[:, 0:n], hT_d[s, kc * 128:(kc + 1) * 128, tok0:tok0 + n],
                [("hT", s, kc)], [hk])
            return ht, hk

        def stats_compute(n, kc, slot):
            ht, hk = slot
            pieces = [(0, min(n, 512))] + ([(512, n)] if n > 512 else [])
            sl = kc % 2
            act(sqb[sl][:, 0:n], ht[:, 0:n], AF.Square, [hk], [("sqb", sl)])
            for pi, (a, b) in enumerate(pieces):
                mm(ST[pi][:, 0:b - a], onesb[:], sqb[sl][:, a:b], kc == 0, kc == KC - 1,
                   ["onesb", ("sqb", sl)], [("ps", "st", pi)], inc=True)

        def stats_step(s, tok0, n, kc):
            stats_compute(n, kc, stats_load(s, tok0, n, kc))

        def stats_finish(n, dst, dkey):
            pieces = [(0, min(n, 512))] + ([(512, n)] if n > 512 else [])
            for pi, (a, b) in enumerate(pieces):
                act(tmp640[:, a:b], ST[pi][:, 0:b - a], AF.Sqrt, [("ps", "st", pi)], ["tmp640"],
                    bias=epsc[:, 0:1], scale=1.0 / D)
            recip(dst[:, 0:n], tmp640[:, 0:n], ["tmp640"], [dkey])

        def rms_stats(s, tok0, n, scale_inv):
            for kc in range(KC):
                stats_step(s, tok0, n, kc)
            stats_finish(n, rstd, "rstd")

        hbf = hb[:].rearrange("p k t -> p (k t)").bitcast(F32)

        def stage0_real(s):
            for tg in range(S // 128):
                par = tg % 2
                xbuf = scrA[:, par * 2048:(par + 1) * 2048]
                stg = hbf[:, par * 2048:(par + 1) * 2048].rearrange("p (k t) -> p k t", k=KC)
                dma("sp", f"x{par}", xbuf, x_d[s, tg * 128:(tg + 1) * 128, :], (), [("xb", par)])
                for k4 in range(4):
                    bank, bkey = next_mm()
                    for kk in range(4):
                        kc = k4 * 4 + kk
                        tr(bank[:, kk * 128:(kk + 1) * 128], xbuf[:, kc * 128:(kc + 1) * 128],
                           [("xb", par)], [bkey], inc=(kk == 3))
                    eng = "dve" if k4 % 2 == 0 else "act"
                    src = bank[:].rearrange("p (k t) -> p k t", k=4)
                    if eng == "dve":
                        cp("dve", stg[:, k4 * 4:(k4 + 1) * 4, :], src, [bkey], [("stg", par)])
                    else:
                        act(stg[:, k4 * 4:(k4 + 1) * 4, :], src, AF.Copy, [bkey], [("stg", par)])
                dst = hT_d[s].rearrange("(k p) t -> p k t", p=128)[:, :, tg * 128:(tg + 1) * 128]
                dma("sp", f"stg{par}", dst, stg, [("stg", par)], [("hT", s, k) for k in range(KC)])

        def epilogue(s):
            fo = VOFF["final_g"]
            for tq in range(S // 256):
                tok0 = tq * 256
                rms_stats(s, tok0, 256, 1.0 / D)
                if os.environ.get("KSTOP", "") == "rms":
                    return
                ostv = scrA[:, 0:4096].rearrange("p (b d) -> p b d", b=2)
                for kc in range(KC):
                    sl = kc % 2
                    ht, hk, hsn = next_hs()
                    dma("sp", hsn, ht[:, 0:256], hT_d[s, kc * 128:(kc + 1) * 128, tok0:tok0 + 256],
                        [("hT", s, kc)], [hk])
                    stt("dve", tf[sl][:, 0:256], ht[:, 0:256], vecs[:, fo + kc:fo + kc + 1], rstd[:, 0:256],
                        ALU.mult, ALU.mult, [hk, "vecs", "rstd"], [("tf", sl)])
                    bank, bkey = next_mm()
                    for tb in range(2):
                        tr(bank[:, tb * 128:(tb + 1) * 128], tf[sl][:, tb * 128:(tb + 1) * 128],
                           [("tf", sl)], [bkey], inc=(tb == 1))
                    src = bank[:, 0:256].rearrange("p (b d) -> p b d", b=2)
                    if kc % 2 == 0:
                        act(ostv[:, :, kc * 128:(kc + 1) * 128], src, AF.Copy, [bkey], ["ost"])
                    else:
                        cp("dve", ostv[:, :, kc * 128:(kc + 1) * 128], src, [bkey], ["ost"])
                for b in range(2):
                    dma("sp", f"ost{b}", out_d[s, tok0 + b * 128:tok0 + (b + 1) * 128, :], ostv[:, b, :],
                        ["ost"], [("out", s)])

        pending_stats = [None]

        def layer_group(s, l, g, nxt=None):
            T0 = g * G
            Wn = min(640, S - T0)
            KST = os.environ.get("KSTOP", "")
            ebv = ebuf.rearrange("p (k t) -> p k t", k=16)
            cyv = cy.rearrange("p (k t) -> p k t", k=8)
            if pending_stats[0] == (s, l, g):
                rs, rsk = rsN, "rsN"
            else:
                rms_stats(s, T0, Wn, 1.0 / D)
                rs, rsk = rstd, "rstd"
            go = VOFF["norm_g"] + l * 16
            for kc in range(KC):
                sl = kc % 2
                ht, hk, hsn = next_hs()
                dma("sp", hsn, ht[:, 0:Wn], hT_d[s, kc * 128:(kc + 1) * 128, T0:T0 + Wn],
                    [("hT", s, kc)], [hk])
                stt("dve", hb[:, kc, 0:Wn], ht[:, 0:Wn], vecs[:, go + kc:go + kc + 1], rs[:, 0:Wn],
                    ALU.mult, ALU.mult, [hk, "vecs", rsk], [("hb", kc)])
            if KST == "hb":
                return
            akeys = [("a", c) for c in range(8)]
            if g == 0:
                memset("pool", abuf[:, :, 0:16], 0.0, akeys)
            else:
                T.op("pool", lambda e: e.tensor_copy(out=abuf[:, :, 0:144], in_=abuf[:, :, 512:656]), akeys, akeys)
            if g == NG - 1:
                memset("pool", abuf[:, :, 528:544], 0.0, akeys)
            if g == 0:
                lead = [(0, 512), (512, 640)]
            elif g == NG - 1:
                lead = [(128, 512)]
            else:
                lead = [(128, 640)]
            main = [(0, 512)]
            hbk = [("hb", kc) for kc in range(KC)]
            for n, (typ, ci) in enumerate(CH):
                if n >= int(os.environ.get("KNCH", 1000)):
                    return
                w, wkey = next_w()
                if os.environ.get("KAFTERW"):
                    return
                if typ == "v":
                    blocks = []
                    for (a, b) in lead:
                        blocks += list(range(a // 128, b // 128))
                    for b0 in range(0, len(blocks), 4):
                        bl = blocks[b0:b0 + 4]
                        bank, bkey = next_mm()
                        for bi, blk in enumerate(bl):
                            for kc in range(KC):
                                mm(bank[:, bi * 128:(bi + 1) * 128], hb[:, kc, blk * 128:(blk + 1) * 128], w[:, kc, :],
                                   kc == 0, kc == KC - 1, [("hb", kc), wkey], [bkey],
                                   inc=(kc == KC - 1 and bi == len(bl) - 1))
                        for bi, blk in enumerate(bl):
                            KB = (T0 + blk * 128) // 128
                            ks = KB % 6
                            cp("dve", vbuf[:, ks, ci * 128:(ci + 1) * 128], bank[:, bi * 128:(bi + 1) * 128],
                               [bkey], [("v", ks)])
                    continue
                pieces = lead if typ in ("k", "ag", "av") else main
                for (a, b) in pieces:
                    nt = b - a
                    bank, bkey = next_mm()
                    for kc in range(KC):
                        mm(bank[:, 0:nt], w[:, kc, :], hb[:, kc, a:b], kc == 0, kc == KC - 1,
                           [("hb", kc), wkey], [bkey], inc=(kc == KC - 1))
                    if os.environ.get("KNOEVAC"):
                        continue
                    if typ == "k":
                        for blk in range(a // 128, b // 128):
                            KB = (T0 + blk * 128) // 128
                            ks = KB % 6
                            o0 = blk * 128 - a
                            if True:
                                cp("dve", kbuf[:, ci, ks, :], bank[:, o0:o0 + 128], [bkey], [("k", ci, ks)])
                            else:
                                act(kbuf[:, ci, ks, :], bank[:, o0:o0 + 128], AF.Copy, [bkey], [("k", ci, ks)])
                    elif typ == "ag":
                        act(sgt[ci % 2][:, a:b], bank[:, 0:nt], AF.Sigmoid, [bkey], [("sgt", ci % 2)])
                    elif typ == "av":
                        tt("dve", abuf[:, ci, 16 + a:16 + b], bank[:, 0:nt], sgt[ci % 2][:, a:b], ALU.mult,
                           [bkey, ("sgt", ci % 2)], [("a", ci)])
                    elif typ == "az":
                        act(ybuf[:, ci, :], bank[:, 0:nt], AF.Silu, [bkey], [("y", ci)])
                    elif typ == "bz":
                        act(ybuf[:, 8 + ci, :], bank[:, 0:nt], AF.Silu, [bkey], [("y", 8 + ci)])
                    elif typ == "q":
                        ts("dve", qbuf[:, ci, :], bank[:, 0:nt], 0.125, ALU.mult, [bkey], [("q", ci)])
            if KST == "inproj":
                return
            cwo = VOFF["conv_w"] + l * 8 * 31
            for c in range(8):
                ds = c % 2
                idb = identb[:]
                cwv = vecs[:, cwo + c * 31:cwo + c * 31 + 31]
                tt(os.environ.get("KDGENG", "dve"), dg[ds][:], bass.AP(idb.tensor, idb.offset, [idb.ap[0], [0, 31], idb.ap[1]]),
                   bass.AP(cwv.tensor, cwv.offset, [cwv.ap[0], cwv.ap[1], [0, 128]]), ALU.mult,
                   ["identb", "vecs"], [("dg", ds)])
                for k in range(31):
                    mm(SC[ds][:, :], dg[ds][:, k, :], abuf[:, c, k + 1:k + 1 + 512], k == 0, k == 30,
                       [("dg", ds), ("a", c)], [("ps", "sc", ds)], inc=(k == 30))
                act(cyv[:, c, :], SC[ds][:, :], AF.Identity, [("ps", "sc", ds)], [("cy", c)],
                    bias=vcol("conv_b", l * 8 + c))
                tt("dve", ysq[ds][:], cyv[:, c, :], cyv[:, c, :], ALU.mult, [("cy", c)], [("ysq", ds)])
                for cp_ in ([c - 1] if c > 0 else []) + ([7] if c == 7 else []):
                    dp = cp_ % 2
                    mm(ST[0][:, :], onesb[:], cyv[:, cp_, :], cp_ == 0, cp_ == 7, ["onesb", ("cy", cp_)], [("ps", "st", 0)], inc=True)
                    mm(ST[1][:, :], onesb[:], ysq[dp][:], cp_ == 0, cp_ == 7, ["onesb", ("ysq", dp)], [("ps", "st", 1)], inc=True)
            ts("dve", meant[:], ST[0][:, :], 1.0 / 1024, ALU.mult, [("ps", "st", 0)], ["meant"])
            tt("dve", tmp640[:, 0:512], meant[:], meant[:], ALU.mult, ["meant"], ["tmp640"])
            stt("dve", tmp640[:, 0:512], ST[1][:, :], 1.0 / 1024, tmp640[:, 0:512], ALU.mult, ALU.subtract,
                [("ps", "st", 1), "tmp640"], ["tmp640"])
            act(tmp640[:, 0:512], tmp640[:, 0:512], AF.Sqrt, ["tmp640"], ["tmp640"], bias=epsc[:, 0:1], scale=1.0)
            recip(lnA[:], tmp640[:, 0:512], ["tmp640"], ["lnA"])
            stt("dve", lnB[:], meant[:], -1.0, lnA[:], ALU.mult, ALU.mult, ["meant", "lnA"], ["lnB"])
            def ln_apply(c):
                sl = c % 2
                tt("dve", tf[sl][:], cyv[:, c, :], lnA[:], ALU.mult, [("cy", c), "lnA"], [("tf", sl)])
                tt("dve", tf[sl][:], tf[sl][:], lnB[:], ALU.add, [("tf", sl), "lnB"], [("tf", sl)])
                act(tb16[sl][:], tf[sl][:], AF.Silu, [("tf", sl), "vecs"], [("tb16", sl)],
                    bias=vcol("cln_b", l * 8 + c), scale=vcol("cln_g", l * 8 + c))
                tt("pool", ybuf[:, c, :], tb16[sl][:], ybuf[:, c, :], ALU.mult, [("tb16", sl), ("y", c)], [("y", c)])
            ln_todo = list(range(8))
            if not os.environ.get("KLNI"):
                while ln_todo:
                    ln_apply(ln_todo.pop(0))
            if KST == "conv":
                return
            it = 0
            for qb in range(4):
                Q = 4 * g + qb
                kbs = [kb for kb in range(3) if 0 <= Q + kb - 1 < S // 128]
                for j in range(4):
                    pts = []
                    for kb in kbs:
                        ks = (Q + kb - 1) % 6
                        scb = it % 2
                        pti = it % 6
                        it += 1
                        sck = ("ps", "sc", scb)
                        for hh in range(4):
                            c = 2 * j + hh // 2
                            hf = hh % 2
                            mm(SC[scb][:, hh * 128:(hh + 1) * 128], identb[:],
                               biasT[:, 4 * j + hh, kb * 128:(kb + 1) * 128],
                               True, False, ["identb", "biasT"], [sck], inc=False)
                            mm(SC[scb][:, hh * 128:(hh + 1) * 128], kbuf[hf * 64:(hf + 1) * 64, j, ks, :],
                               qbuf[hf * 64:(hf + 1) * 64, c, qb * 128:(qb + 1) * 128], False, True,
                               [("k", j, ks), ("q", c)], [sck], inc=(hh == 3))
                        act(PT[pti][:], SC[scb][:, :], AF.Exp, [sck], [("pt", pti)])
                        pts.append((pti, ks))
                    pvk = ("ps", "pv")
                    if os.environ.get("KATT") == "s":
                        continue
                    for cc in range(2):
                        for hf in range(2):
                            hh = 2 * cc + hf
                            for i, (pti, ks) in enumerate(pts):
                                mm(PV[hf * 64:(hf + 1) * 64, cc * 128:(cc + 1) * 128], vbuf[:, ks, j * 64:(j + 1) * 64],
                                   PT[pti][:, hh * 128:(hh + 1) * 128], i == 0, i == len(pts) - 1,
                                   [("v", ks), ("pt", pti)], [pvk], inc=False, tp=((0, 64) if hf else None))
                            for i, (pti, ks) in enumerate(pts):
                                mm(PV[hf * 64:(hf + 1) * 64, 256 + cc * 128:256 + (cc + 1) * 128], onesb[:, 0:64],
                                   PT[pti][:, hh * 128:(hh + 1) * 128], i == 0, i == len(pts) - 1,
                                   ["onesb", ("pt", pti)], [pvk],
                                   inc=(cc == 1 and hf == 1 and i == len(pts) - 1), tp=((0, 64) if hf else None))
                    dsl = (qb * 4 + j) % 2
                    for cc in range(2):
                        ts("dve", dens[dsl][:, cc * 128:(cc + 1) * 128], PV[:, 256 + cc * 128:256 + (cc + 1) * 128],
                           esink[:, l * 8 + j * 2 + cc:l * 8 + j * 2 + cc + 1], ALU.add, [pvk, "esink"], [("dens", dsl)])
                    recip(dens[dsl][:], dens[dsl][:], [("dens", dsl)], [("dens", dsl)])
                    yv = ybuf[:, 8 + 2 * j:8 + 2 * j + 2, qb * 128:(qb + 1) * 128]
                    yk = [("y", 8 + 2 * j), ("y", 8 + 2 * j + 1)]
                    tt("dve", rz[dsl][:].rearrange("p (c q) -> p c q", c=2), dens[dsl][:].rearrange("p (c q) -> p c q", c=2),
                       yv, ALU.mult, [("dens", dsl)] + yk, [("rz", dsl)])
                    tt("dve", yv, PV[:, 0:256].rearrange("p (c q) -> p c q", c=2),
                       rz[dsl][:].rearrange("p (c q) -> p c q", c=2), ALU.mult, [pvk, ("rz", dsl)], yk)
                    if ln_todo:
                        ln_apply(ln_todo.pop(0))
            while ln_todo:
                ln_apply(ln_todo.pop(0))
            if KST == "attn":
                return
            LA = 2
            pre = {}

            def pf(dc):
                ht, hk, hsn = next_hs()
                dma("sp", hsn, ht[:, 0:G], hT_d[s, dc * 128:(dc + 1) * 128, T0:T0 + G], [("hT", s, dc)], [hk])
                pre[dc] = (ht, hk, hsn)
            for dc in range(LA):
                pf(dc)
            spre = {}
            if nxt is not None:
                T0n = nxt[2] * G
                Wnn = min(640, S - T0n)
                for dc in range(LA):
                    spre[dc] = stats_load(nxt[0], T0n, Wnn, dc)
            for dc in range(16):
                w, wkey = next_w()
                bank, bkey = next_mm()
                for kc in range(KC):
                    mm(bank[:, :], w[:, kc, :], ybuf[:, kc, :], kc == 0, kc == KC - 1, [("y", kc), wkey], [bkey],
                       inc=(kc == KC - 1))
                if dc + LA < 16:
                    pf(dc + LA)
                if nxt is not None:
                    if dc + LA < 16:
                        spre[dc + LA] = stats_load(nxt[0], T0n, Wnn, dc + LA)
                    stats_compute(Wnn, dc, spre.pop(dc))
                ht, hk, hsn = pre.pop(dc)
                tt("dve", ht[:, 0:G], ht[:, 0:G], bank[:, :], ALU.add, [hk, bkey], [hk])
                act(hb[:, dc, 0:G], ht[:, 0:G], AF.Copy, [hk], [("hb", dc)])
                dma("sp", hsn, hT_d[s, dc * 128:(dc + 1) * 128, T0:T0 + G], ht[:, 0:G], [hk], [("hT", s, dc)])
            if nxt is not None:
                stats_finish(Wnn, rsN, "rsN")
                pending_stats[0] = nxt
            if KST == "outproj":
                return
            dma("sp", "ptile", ptile[:], p_d[l, s, T0:T0 + G, :].rearrange("(b p) f -> p b f", p=128), (), ["ptile"])
            for pc in range(2):
                for tb in range(4):
                    tr(ST[pc][:, tb * 128:(tb + 1) * 128], ptile[:, tb, pc * 128:(pc + 1) * 128], ["ptile"],
                       [("ps", "st", pc)], inc=(tb == 3))
                cp("dve", pT[:, pc, :], ST[pc][:, :], [("ps", "st", pc)], [("pT", pc)])
            for dc in range(16):
                bank, bkey = next_mm()
                for pc in range(2):
                    mm(bank[:, :], wpe[:, pc, dc * 128:(dc + 1) * 128], pT[:, pc, :], pc == 0, pc == 1,
                       ["wpe", ("pT", pc)], [bkey], inc=(pc == 1))
                sq4 = [(ysq[0], ("ysq", 0)), (ysq[1], ("ysq", 1)), (tb16[0], ("tb16", 0)), (tb16[1], ("tb16", 1))]
                sqt, sqk = sq4[dc % 4]
                act(ebv[:, dc, :], bank[:, :], AF.Copy, [bkey], [("e", dc)])
                tt("dve", sqt[:], ebv[:, dc, :], ebv[:, dc, :], ALU.mult, [("e", dc)], [sqk])
                for dp in ([dc - 3] if dc >= 3 else []) + ([13, 14, 15] if dc == 15 else []):
                    sqt2, sqk2 = sq4[dp % 4]
                    mm(ST[0][:, :], onesb[:], sqt2[:], dp == 0, dp == 15, ["onesb", sqk2], [("ps", "st", 0)], inc=True)
            act(tmp640[:, 0:512], ST[0][:, :], AF.Sqrt, [("ps", "st", 0)], ["tmp640"], bias=epsc[:, 0:1], scale=1.0 / D)
            recip(rstd[:, 0:512], tmp640[:, 0:512], ["tmp640"], ["rstd"])
            pgo = VOFF["pe_g"] + l * 16
            pre2 = {}

            def pf2(dc):
                ht, hk, hsn = next_hs()
                dma("sp", hsn, ht[:, 0:G], hT_d[s, dc * 128:(dc + 1) * 128, T0:T0 + G], [("hT", s, dc)], [hk])
                pre2[dc] = (ht, hk, hsn)
            for dc in range(LA):
                pf2(dc)
            for dc in range(16):
                w, wkey = next_w()
                bank, bkey = next_mm()
                for kc in range(KC):
                    mm(bank[:, :], w[:, kc, :], hb[:, kc, 0:G], kc == 0, kc == KC - 1, [("hb", kc), wkey], [bkey],
                       inc=(kc == KC - 1))
                if dc + LA < 16:
                    pf2(dc + LA)
                sl = dc % 2
                act(sgb[sl][:], bank[:, :], AF.Sigmoid, [bkey], [("sg", sl)])
                tt("dve", tf[sl][:], ebv[:, dc, :], rstd[:, 0:512], ALU.mult, [("e", dc), "rstd"], [("tf", sl)])
                stt("dve", tf[sl][:], tf[sl][:], vecs[:, pgo + dc:pgo + dc + 1], sgb[sl][:], ALU.mult, ALU.mult,
                    [("tf", sl), "vecs", ("sg", sl)], [("tf", sl)])
                ht, hk, hsn = pre2.pop(dc)
                tt("dve", ht[:, 0:G], ht[:, 0:G], tf[sl][:], ALU.add, [hk, ("tf", sl)], [hk])
                dma("sp", hsn, hT_d[s, dc * 128:(dc + 1) * 128, T0:T0 + G], ht[:, 0:G], [hk], [("hT", s, dc)])

        KSTOP = os.environ.get("KSTOP", "")
        convert(layers[0], 0, NT) if len(layers) else None
        for s in range(nseq):
            if KSTOP == "setup":
                break
            if do_stage0:
                stage0_real(s)
                T.barrier()
            if KSTOP == "stage0":
                break
            for l in layers:
                for pc in range(2):
                    if os.environ.get("KNOWPE"):
                        continue
                    dma("pool", f"wpe{pc}", wpe[:, pc, :], wpe_d[l, :, pc, :], (), ["wpe"], mld=int(os.environ.get("KMLD", 512)))
                for g in range(int(os.environ.get("KGROUPS", NG))):
                    li = layers.index(l)
                    if s == 0 and li + 1 < len(layers) and g == 0:
                        while cv_pending:
                            l2, i2 = cv_pending.pop(0)
                            convert(l2, i2, i2 + 1)
                        cv_pending.extend((layers[li + 1], i) for i in range(NT))
                    ng = int(os.environ.get("KGROUPS", NG))
                    if g + 1 < ng:
                        nxt = (s, l, g + 1)
                    elif li + 1 < len(layers) and ng == NG:
                        nxt = (s, layers[li + 1], 0)
                    else:
                        nxt = None
                    layer_group(s, l, g, nxt)
            while cv_pending:
                l2, i2 = cv_pending.pop(0)
                convert(l2, i2, i2 + 1)
            T.barrier()
            if do_final:
                epilogue(s)
                T.barrier()
        T.barrier()

        names = T.sem_names()
        sems = {n: es.enter_context(nc.semaphore("s_" + n)) for n in names}
        with nc.Block() as block:
            def emit(kind):
                def run(e):
                    for waits, fn, inc in T.ops[kind]:
                        for sname, v in waits:
                            e.wait_ge(sems[sname], v)
                        ins = fn(e)
                        if inc is not None:
                            ins.then_inc(sems[inc[0]], inc[1])
                    for sname, v in T.pending[kind].items():
                        e.wait_ge(sems[sname], v)
                return run
            block.tensor(emit("pe"))
            block.scalar(emit("act"))
            block.vector(emit("dve"))
            block.gpsimd(emit("pool"))
            block.sync(emit("sp"))
    return nc


def prep_shared(norm_g, w_in, conv_w, conv_b, cln_g, cln_b, sink, rel_bias, w_out, w_pe, pe_g, w_pg, final_g):
    f = lambda a: np.ascontiguousarray(np.asarray(a, dtype=np.float32))
    w_in, w_out, w_pe, w_pg = f(w_in), f(w_out), f(w_pe), f(w_pg)
    idx = _chunk_cols()
    wi = w_in[:, :, idx.reshape(-1)].reshape(NL, KC, 128, NCH, 128)
    def padk(a):
        o = np.zeros(a.shape[:3] + (KC + 1, 128), np.float32)
        o[:, :, :, :KC, :] = a
        return o
    wi = padk(wi.transpose(0, 3, 2, 1, 4))
    wo = padk(w_out.reshape(NL, KC, 128, 16, 128).transpose(0, 3, 2, 1, 4))
    wg = padk(w_pg.reshape(NL, KC, 128, 16, 128).transpose(0, 3, 2, 1, 4))
    wp = np.ascontiguousarray(w_pe.reshape(NL, 2, 128, D).transpose(0, 2, 1, 3))
    vecs = _pack_vecs(f(norm_g), f(pe_g), f(final_g), f(conv_b), f(cln_g), f(cln_b), f(conv_w), f(sink))
    buckets, band = _t5_band_buckets()
    gathered = f(rel_bias)[buckets]
    biasg = np.ascontiguousarray(gathered.reshape(128, 3, 128, 16).transpose(3, 2, 1, 0)).reshape(16, 128, 384)
    m01 = band.astype(np.float32).reshape(128, 3, 128).transpose(2, 1, 0).reshape(128, 384)
    masks = np.ascontiguousarray(np.stack([m01, (m01 - 1.0) * 30000.0]).astype(np.float32))
    return {"w_in": wi, "w_out": wo, "w_pg": wg, "w_pe": wp, "vecs": vecs, "biasg": biasg, "masks": masks,
            "ident": np.eye(128, dtype=np.float32)}


def kernel(x, p, norm_g, w_in, conv_w, conv_b, cln_g, cln_b, sink, rel_bias, w_out, w_pe, pe_g, w_pg, final_g):
    x = np.asarray(x, dtype=np.float32)
    p = np.asarray(p, dtype=np.float32)
    shared = prep_shared(norm_g, w_in, conv_w, conv_b, cln_g, cln_b, sink, rel_bias, w_out, w_pe, pe_g, w_pg, final_g)
    nc = build_program()
    in_maps = []
    for c in range(NCORES):
        m = dict(shared)
        m["x"] = np.ascontiguousarray(x[c * SEQ_PER_CORE:(c + 1) * SEQ_PER_CORE])
        m["p"] = np.ascontiguousarray(p[:, c * SEQ_PER_CORE:(c + 1) * SEQ_PER_CORE])
        in_maps.append(m)
    res = run_bass_kernel_spmd(nc, in_maps, core_ids=list(range(NCORES)))
    out = np.concatenate([r["out"] for r in res.results], axis=0)
    return out.astype(np.float32)
```

```python
import os
import numpy as np
import concourse.bass as bass
import concourse.mybir as mybir
from concourse.bass_utils import run_bass_kernel_spmd

F32 = mybir.dt.float32
BF16 = mybir.dt.bfloat16
AF = mybir.ActivationFunctionType
ALU = mybir.AluOpType

D = 2048
S = 2048
NL = 4
G = 512
NG = S // G
KC = 16
EPS = 1e-6
NCORES = 8
SEQ_PER_CORE = 2
NWS = int(os.environ.get("KNWS", 3))

CH = []
for _j in range(4):
    CH.append(("k", _j))
for _vc in range(2):
    CH.append(("v", _vc))
for _c in range(8):
    CH.append(("ag", _c))
    CH.append(("av", _c))
for _c in range(8):
    CH.append(("az", _c))
for _c in range(8):
    CH.append(("q", _c))
for _c in range(8):
    CH.append(("bz", _c))
NCH = len(CH)


def _chunk_cols():
    AV0, AG0, AZ0, Q0, K0, V0, BZ0 = 0, 1024, 2048, 3072, 4096, 4352, 4608
    idx = np.zeros((NCH, 128), np.int64)
    r = np.arange(128)
    for n, (t, i) in enumerate(CH):
        if t == "k":
            idx[n] = K0 + i * 64 + (r % 64)
        elif t == "v":
            idx[n] = V0 + i * 128 + r
        elif t == "ag":
            idx[n] = AG0 + i * 128 + r
        elif t == "av":
            idx[n] = AV0 + i * 128 + r
        elif t == "az":
            idx[n] = AZ0 + i * 128 + r
        elif t == "q":
            idx[n] = Q0 + i * 128 + r
        elif t == "bz":
            idx[n] = BZ0 + i * 128 + r
    return idx


VOFF = {}
_o = 0
for _name, _n in (("norm_g", NL * 16), ("pe_g", NL * 16), ("final_g", 16), ("conv_b", NL * 8),
                  ("cln_g", NL * 8), ("cln_b", NL * 8), ("conv_w", NL * 8 * 31), ("sink", NL * 8)):
    VOFF[_name] = _o
    _o += _n
NV = _o


def _pack_vecs(norm_g, pe_g, final_g, conv_b, cln_g, cln_b, conv_w, sink):
    v = np.zeros((128, NV), np.float32)
    def put(name, arr):
        v[:, VOFF[name]:VOFF[name] + arr.shape[1]] = arr
    put("norm_g", norm_g.reshape(NL, 16, 128).transpose(2, 0, 1).reshape(128, -1))
    put("pe_g", pe_g.reshape(NL, 16, 128).transpose(2, 0, 1).reshape(128, -1))
    put("final_g", final_g.reshape(16, 128).T)
    put("conv_b", conv_b.reshape(NL, 8, 128).transpose(2, 0, 1).reshape(128, -1))
    put("cln_g", cln_g.reshape(NL, 8, 128).transpose(2, 0, 1).reshape(128, -1))
    put("cln_b", cln_b.reshape(NL, 8, 128).transpose(2, 0, 1).reshape(128, -1))
    put("conv_w", conv_w.reshape(NL, 31, 8, 128).transpose(3, 0, 2, 1).reshape(128, -1))
    sk = np.zeros((128, NL, 4, 2), np.float32)
    for j in range(4):
        for cc in range(2):
            sk[:64, :, j, cc] = sink[:, 4 * j + 2 * cc][None, :]
            sk[64:, :, j, cc] = sink[:, 4 * j + 2 * cc + 1][None, :]
    put("sink", sk.reshape(128, -1))
    return v


def _t5_band_buckets():
    BLK, NUM_BUCKETS, MAX_DISTANCE, WINDOW = 128, 32, 128, 128
    q_off = np.arange(BLK)[:, None]
    k_off = np.arange(3 * BLK)[None, :] - BLK
    rel = k_off - q_off
    half = NUM_BUCKETS // 2
    ret = (rel > 0).astype(np.int32) * half
    n = np.abs(rel)
    max_exact = half // 2
    large = max_exact + (np.log(np.maximum(n, 1) / max_exact)
                         / np.log(MAX_DISTANCE / max_exact) * (half - max_exact)).astype(np.int32)
    large = np.minimum(large, half - 1)
    ret = ret + np.where(n < max_exact, n, large)
    return ret.astype(np.int32), (n <= WINDOW)


class Tracker:
    ENG = ("pe", "act", "dve", "pool", "sp")

    def __init__(self):
        self.ops = {e: [] for e in self.ENG}
        self.cnt = {e: 0 for e in self.ENG}
        self.dcnt = {}
        self.last_w = {}
        self.rd = {}
        self.seen = {e: {} for e in self.ENG}
        self.pending = {e: {} for e in self.ENG}

    def _deps(self, eng, reads, writes):
        deps = dict(self.pending[eng])
        self.pending[eng] = {}

        def add(ev):
            if ev is None:
                return
            s, v = ev
            if deps.get(s, 0) < v:
                deps[s] = v
        for k in reads:
            add(self.last_w.get(k))
        for k in writes:
            add(self.last_w.get(k))
            for ev in self.rd.get(k, ()):
                add(ev)
        out = []
        seen = self.seen[eng]
        for s, v in deps.items():
            if s == "pe" and eng == "pe":
                continue
            if seen.get(s, 0) >= v:
                continue
            seen[s] = v
            out.append((s, v))
        return out

    def _record(self, ev, reads, writes):
        for k in writes:
            self.last_w[k] = ev
            self.rd[k] = []
        for k in reads:
            self.rd.setdefault(k, []).append(ev)

    def op(self, eng, fn, reads=(), writes=(), inc=True):
        waits = self._deps(eng, reads, writes)
        if inc:
            self.cnt[eng] += 1
            ev = (eng, self.cnt[eng])
        else:
            ev = (eng, self.cnt[eng] + 1)
        self._record(ev, reads, writes)
        self.ops[eng].append((waits, fn, (eng, 1) if inc else None))

    def dma(self, eng, sem, fn, reads=(), writes=()):
        waits = self._deps(eng, reads, writes)
        self.dcnt[sem] = self.dcnt.get(sem, 0) + 16
        ev = (sem, self.dcnt[sem])
        self._record(ev, reads, writes)
        self.ops[eng].append((waits, fn, (sem, 16)))

    def barrier(self):
        allv = dict(self.cnt)
        allv.update(self.dcnt)
        for e in self.ENG:
            for s, v in allv.items():
                if v > 0 and self.pending[e].get(s, 0) < v:
                    self.pending[e][s] = v

    def sem_names(self):
        return list(self.ENG) + sorted(self.dcnt.keys())


def build_program(nseq=SEQ_PER_CORE, layers=(0, 1, 2, 3), do_stage0=True, do_final=True, debug_h=False):
    nc = bass.Bass("TRN2", target_bir_lowering=False)
    T = Tracker()

    x_d = nc.dram_tensor("x", [nseq, S, D], F32, kind="ExternalInput").ap()
    p_d = nc.dram_tensor("p", [NL, nseq, S, 256], F32, kind="ExternalInput").ap()
    win_d = nc.dram_tensor("w_in", [NL, NCH, 128, KC + 1, 128], F32, kind="ExternalInput").ap()
    wout_d = nc.dram_tensor("w_out", [NL, 16, 128, KC + 1, 128], F32, kind="ExternalInput").ap()
    wpg_d = nc.dram_tensor("w_pg", [NL, 16, 128, KC + 1, 128], F32, kind="ExternalInput").ap()
    wpe_d = nc.dram_tensor("w_pe", [NL, 128, 2, D], F32, kind="ExternalInput").ap()
    vecs_d = nc.dram_tensor("vecs", [128, NV], F32, kind="ExternalInput").ap()
    bias_d = nc.dram_tensor("biasg", [16, 128, 384], F32, kind="ExternalInput").ap()
    mask_d = nc.dram_tensor("masks", [2, 128, 384], F32, kind="ExternalInput").ap()
    ident_d = nc.dram_tensor("ident", [128, 128], F32, kind="ExternalInput").ap()
    out_d = nc.dram_tensor("out", [nseq, S, D], F32, kind="ExternalOutput").ap()
    if debug_h:
        hT_t = nc.dram_tensor("hT", [nseq, D, S], F32, kind="ExternalOutput")
    else:
        hT_t = nc.dram_tensor("hT", [nseq, D, S], F32)
    hT_d = hT_t.ap()
    wbin_d = nc.dram_tensor("wb_in", [NL, NCH, 128, KC * 128], BF16).ap()
    wbout_d = nc.dram_tensor("wb_out", [NL, 16, 128, KC * 128], BF16).ap()
    wbpg_d = nc.dram_tensor("wb_pg", [NL, 16, 128, KC * 128], BF16).ap()

    from contextlib import ExitStack
    es = ExitStack()

    def sb(name, shape, dt):
        return es.enter_context(nc.sbuf_tensor(name, shape, dt))

    def ps(name):
        return es.enter_context(nc.psum_tensor(name, [128, 512], F32))

    with es:
        identf = sb("identf", [128, 128], F32)
        identb = sb("identb", [128, 128], BF16)
        onesb = sb("onesb", [128, 128], BF16)
        vecs = sb("vecs_sb", [128, NV], F32)
        esink = sb("esink", [128, NL * 8], F32)
        epsc = sb("epsc", [128, 1], F32)
        biasT = sb("biasT", [128, 16, 384], BF16)
        NHS = int(os.environ.get("KNHS", 8))
        hs = [sb(f"hs{i}", [128, 640], F32) for i in range(NHS)]
        hs_i = [0]

        def next_hs():
            i = hs_i[0] % NHS
            hs_i[0] += 1
            return hs[i], ("hs", i), f"hs{i}"
        sqb = [sb(f"sqb{i}", [128, 640], BF16) for i in range(2)]
        hb = sb("hb", [128, KC, 640], BF16)
        rstd = sb("rstd", [128, 640], F32)
        rsN = sb("rsN", [128, 640], F32)
        tmp640 = sb("tmp640", [128, 640], F32)
        wsl = [sb(f"w{i}", [128, KC, 128], BF16) for i in range(NWS)]
        wpe = sb("wpe", [128, 2, D], BF16)
        abuf = sb("abuf", [128, 8, 656], BF16)
        kbuf = sb("kbuf", [128, 4, 6, 128], BF16)
        vbuf = sb("vbuf", [128, 6, 256], BF16)
        qbuf = sb("qbuf", [128, 8, 512], BF16)
        ybuf = sb("ybuf", [128, 16, 512], BF16)
        dg = [sb(f"dg{i}", [128, 31, 128], BF16) for i in range(2)]
        scrA = sb("scrA", [128, 6144], F32)
        ysq = [sb(f"ysq{i}", [128, 512], BF16) for i in range(2)]
        meant = sb("meant", [128, 512], F32)
        lnA = sb("lnA", [128, 512], F32)
        lnB = sb("lnB", [128, 512], F32)
        tf = [sb(f"tf{i}", [128, 512], F32) for i in range(2)]
        tb16 = [sb(f"tb16_{i}", [128, 512], BF16) for i in range(2)]
        PT = [sb(f"pt{i}", [128, 512], BF16) for i in range(6)]
        dens = [sb(f"dens{i}", [128, 256], F32) for i in range(2)]
        rz = [sb(f"rz{i}", [128, 256], F32) for i in range(2)]
        ptile = sb("ptile", [128, 4, 256], F32)
        pT = sb("pT", [128, 2, 512], BF16)
        sgb = [sb(f"sg{i}", [128, 512], F32) for i in range(2)]
        sgt = [sb(f"sgt{i}", [128, 640], BF16) for i in range(2)]

        ebuf = scrA[:, 0:4096].bitcast(BF16)
        cy = scrA[:, 4096:6144].bitcast(BF16)
        xl = scrA[:, 0:3072]
        hst = scrA[:, 3072:6144]
        ost = scrA

        if os.environ.get("KVERBOSE"):
            print("SBUF_REMAINING", nc.sbuf_bytes_remaining)
        MM = [ps(f"mm{i}") for i in range(3)]
        ST = [ps(f"st{i}") for i in range(2)]
        SC = [ps(f"sc{i}") for i in range(2)]
        PV = ps("pv")

        mm_i = [0]

        def next_mm():
            i = mm_i[0] % 3
            mm_i[0] += 1
            return MM[i], ("ps", "mm", i)

        eng_of = {"pe": "tensor", "act": "scalar", "dve": "vector", "pool": "gpsimd", "sp": "sync"}

        def mm(out, lhsT, rhs, start, stop, reads, writes, inc, tp=None):
            if tp is None:
                T.op("pe", lambda e: e.matmul(out, lhsT, rhs, start=start, stop=stop), reads, writes, inc)
            else:
                T.op("pe", lambda e: e.matmul(out, lhsT, rhs, start=start, stop=stop, tile_position=tp),
                     reads, writes, inc)

        def tr(out, in_, reads, writes, inc):
            T.op("pe", lambda e: e.transpose(out, in_, identf[:]), list(reads) + ["identf"], writes, inc)

        def act(out, in_, func, reads, writes, bias=None, scale=None):
            kw = {}
            if bias is not None:
                kw["bias"] = bias
            if scale is not None:
                kw["scale"] = scale
            T.op("act", lambda e: e.activation(out=out, in_=in_, func=func, **kw), reads, writes)

        def tt(eng, out, in0, in1, op, reads, writes):
            T.op(eng, lambda e: e.tensor_tensor(out=out, in0=in0, in1=in1, op=op), reads, writes)

        def ts(eng, out, in0, s1, op0, reads, writes, s2=None, op1=None):
            if op1 is None:
                T.op(eng, lambda e: e.tensor_scalar(out=out, in0=in0, scalar1=s1, scalar2=None, op0=op0),
                     reads, writes)
            else:
                T.op(eng, lambda e: e.tensor_scalar(out=out, in0=in0, scalar1=s1, scalar2=s2, op0=op0, op1=op1),
                     reads, writes)

        def stt(eng, out, in0, scalar, in1, op0, op1, reads, writes):
            T.op(eng, lambda e: e.scalar_tensor_tensor(out=out, in0=in0, scalar=scalar, in1=in1, op0=op0, op1=op1),
                 reads, writes)

        def cp(eng, out, in_, reads, writes):
            T.op(eng, lambda e: e.tensor_copy(out=out, in_=in_), reads, writes)

        def recip(out, in_, reads, writes):
            T.op("dve", lambda e: e.reciprocal(out=out, in_=in_), reads, writes)

        def memset(eng, ap, val, writes):
            T.op(eng, lambda e: e.memset(ap, val), (), writes)

        def dma(eng, sem, out, in_, reads, writes, mld=None):
            if mld is None:
                T.dma(eng, sem, lambda e: e.dma_start(out=out, in_=in_), reads, writes)
            else:
                T.dma(eng, sem, lambda e: e.dma_start(out=out, in_=in_, max_dma_last_dim=mld), reads, writes)

        def vcol(name, idx):
            o = VOFF[name] + idx
            return vecs[:, o:o + 1]

        def layer_tiles(l):
            out = []
            for n in range(NCH):
                out.append((wbin_d[l, n], win_d[l, n, :, 0:KC, :].rearrange("p k c -> p (k c)"), ("wb", l, "i", n)))
            for dc in range(16):
                out.append((wbout_d[l, dc], wout_d[l, dc, :, 0:KC, :].rearrange("p k c -> p (k c)"), ("wb", l, "o", dc)))
            for dc in range(16):
                out.append((wbpg_d[l, dc], wpg_d[l, dc, :, 0:KC, :].rearrange("p k c -> p (k c)"), ("wb", l, "g", dc)))
            return out
        LT = {l: layer_tiles(l) for l in layers}
        NT = NCH + 32
        wseq = []
        for s in range(nseq):
            for l in layers:
                for g in range(NG):
                    wseq.extend(LT[l])
        w_issued = [0]
        w_used = [0]
        cv_i = [0]
        NCV = 12
        cv_pending = []

        def convert(l, i0, i1):
            for i in range(i0, min(i1, NT)):
                dst, src, key = LT[l][i]
                cs = cv_i[0] % NCV
                cv_i[0] += 1
                dma("pool", f"cv{cs}", dst, src, (), [key, ("cvsem", cs)], mld=512)

        def issue_w():
            i = w_issued[0]
            if i >= len(wseq):
                return
            sl = i % NWS
            src, _, key = wseq[i]
            dma("pool", f"w{sl}", wsl[sl][:].rearrange("p k c -> p (k c)"), src, [key], [("w", sl)])
            w_issued[0] += 1
            if cv_pending and w_issued[0] % 3 == 0:
                l2, i2 = cv_pending.pop(0)
                convert(l2, i2, i2 + 1)

        def next_w():
            i = w_used[0]
            while w_issued[0] < min(len(wseq), i + NWS):
                issue_w()
            w_used[0] += 1
            sl = i % NWS
            return wsl[sl], ("w", sl)

        dma("sp", "c0", identf[:], ident_d, (), ["identf"])
        dma("sp", "c1", vecs[:], vecs_d, (), ["vecs"])
        cp("dve", identb[:], identf[:], ["identf"], ["identb"])
        memset("dve", onesb[:], 1.0, ["onesb"])
        memset("dve", epsc[:], EPS, ["epsc"])
        so = VOFF["sink"]
        act(esink[:], vecs[:, so:so + NL * 8], AF.Exp, ["vecs"], ["esink"])
        m0, m0k, m0s = next_hs()
        m1, m1k, m1s = next_hs()
        dma("sp", m0s, m0[:, 0:384], mask_d[0], (), [m0k])
        dma("sp", m1s, m1[:, 0:384], mask_d[1], (), [m1k])
        for h in range(16):
            bt, btk, bts = hs[2 + h % 2], ("hs", 2 + h % 2), f"hs{2 + h % 2}"
            dma("sp", bts, bt[:, 0:384], bias_d[h], (), [btk])
            tt("dve", tf[h % 2][:, 0:384], bt[:, 0:384], m0[:, 0:384], ALU.mult,
               [btk, m0k], [("tf", h % 2)])
            tt("dve", biasT[:, h, :], tf[h % 2][:, 0:384], m1[:, 0:384], ALU.add,
               [("tf", h % 2), m1k], ["biasT"])
        T.barrier()

        def stats_load(s, tok0, n, kc):
            ht, hk, hsn = next_hs()
            dma("sp", hsn, ht[:, 0:n], hT_d[s, kc * 128:(kc + 1) * 128, tok0:tok0 + n],
                [("hT", s, kc)], [hk])
            return ht, hk

        def stats_compute(n, kc, slot):
            ht, hk = slot
            pieces = [(0, min(n, 512))] + ([(512, n)] if n > 512 else [])
            sl = kc % 2
            act(sqb[sl][:, 0:n], ht[:, 0:n], AF.Square, [hk], [("sqb", sl)])
            for pi, (a, b) in enumerate(pieces):
                mm(ST[pi][:, 0:b - a], onesb[:], sqb[sl][:, a:b], kc == 0, kc == KC - 1,
                   ["onesb", ("sqb", sl)], [("ps", "st", pi)], inc=True)

        def stats_step(s, tok0, n, kc):
            stats_compute(n, kc, stats_load(s, tok0, n, kc))

        def stats_finish(n, dst, dkey):
            pieces = [(0, min(n, 512))] + ([(512, n)] if n > 512 else [])
            for pi, (a, b) in enumerate(pieces):
                act(tmp640[:, a:b], ST[pi][:, 0:b - a], AF.Sqrt, [("ps", "st", pi)], ["tmp640"],
                    bias=epsc[:, 0:1], scale=1.0 / D)
            recip(dst[:, 0:n], tmp640[:, 0:n], ["tmp640"], [dkey])

        def rms_stats(s, tok0, n, scale_inv):
            for kc in range(KC):
                stats_step(s, tok0, n, kc)
            stats_finish(n, rstd, "rstd")

        hbf = hb[:].rearrange("p k t -> p (k t)").bitcast(F32)

        def stage0_real(s):
            for tg in range(S // 128):
                par = tg % 2
                xbuf = scrA[:, par * 2048:(par + 1) * 2048]
                stg = hbf[:, par * 2048:(par + 1) * 2048].rearrange("p (k t) -> p k t", k=KC)
                dma("sp", f"x{par}", xbuf, x_d[s, tg * 128:(tg + 1) * 128, :], (), [("xb", par)])
                for k4 in range(4):
                    bank, bkey = next_mm()
                    for kk in range(4):
                        kc = k4 * 4 + kk
                        tr(bank[:, kk * 128:(kk + 1) * 128], xbuf[:, kc * 128:(kc + 1) * 128],
                           [("xb", par)], [bkey], inc=(kk == 3))
                    eng = "dve" if k4 % 2 == 0 else "act"
                    src = bank[:].rearrange("p (k t) -> p k t", k=4)
                    if eng == "dve":
                        cp("dve", stg[:, k4 * 4:(k4 + 1) * 4, :], src, [bkey], [("stg", par)])
                    else:
                        act(stg[:, k4 * 4:(k4 + 1) * 4, :], src, AF.Copy, [bkey], [("stg", par)])
                dst = hT_d[s].rearrange("(k p) t -> p k t", p=128)[:, :, tg * 128:(tg + 1) * 128]
                dma("sp", f"stg{par}", dst, stg, [("stg", par)], [("hT", s, k) for k in range(KC)])

        def epilogue(s):
            fo = VOFF["final_g"]
            for tq in range(S // 256):
                tok0 = tq * 256
                rms_stats(s, tok0, 256, 1.0 / D)
                if os.environ.get("KSTOP", "") == "rms":
                    return
                ostv = scrA[:, 0:4096].rearrange("p (b d) -> p b d", b=2)
                for kc in range(KC):
                    sl = kc % 2
                    ht, hk, hsn = next_hs()
                    dma("sp", hsn, ht[:, 0:256], hT_d[s, kc * 128:(kc + 1) * 128, tok0:tok0 + 256],
                        [("hT", s, kc)], [hk])
                    stt("dve", tf[sl][:, 0:256], ht[:, 0:256], vecs[:, fo + kc:fo + kc + 1], rstd[:, 0:256],
                        ALU.mult, ALU.mult, [hk, "vecs", "rstd"], [("tf", sl)])
                    bank, bkey = next_mm()
                    for tb in range(2):
                        tr(bank[:, tb * 128:(tb + 1) * 128], tf[sl][:, tb * 128:(tb + 1) * 128],
                           [("tf", sl)], [bkey], inc=(tb == 1))
                    src = bank[:, 0:256].rearrange("p (b d) -> p b d", b=2)
                    if kc % 2 == 0:
                        act(ostv[:, :, kc * 128:(kc + 1) * 128], src, AF.Copy, [bkey], ["ost"])
                    else:
                        cp("dve", ostv[:, :, kc * 128:(kc + 1) * 128], src, [bkey], ["ost"])
                for b in range(2):
                    dma("sp", f"ost{b}", out_d[s, tok0 + b * 128:tok0 + (b + 1) * 128, :], ostv[:, b, :],
                        ["ost"], [("out", s)])

        pending_stats = [None]

        def layer_group(s, l, g, nxt=None):
            T0 = g * G
            Wn = min(640, S - T0)
            KST = os.environ.get("KSTOP", "")
            ebv = ebuf.rearrange("p (k t) -> p k t", k=16)
            cyv = cy.rearrange("p (k t) -> p k t", k=8)
            if pending_stats[0] == (s, l, g):
                rs, rsk = rsN, "rsN"
            else:
                rms_stats(s, T0, Wn, 1.0 / D)
                rs, rsk = rstd, "rstd"
            go = VOFF["norm_g"] + l * 16
            for kc in range(KC):
                sl = kc % 2
                ht, hk, hsn = next_hs()
                dma("sp", hsn, ht[:, 0:Wn], hT_d[s, kc * 128:(kc + 1) * 128, T0:T0 + Wn],
                    [("hT", s, kc)], [hk])
                stt("dve", hb[:, kc, 0:Wn], ht[:, 0:Wn], vecs[:, go + kc:go + kc + 1], rs[:, 0:Wn],
                    ALU.mult, ALU.mult, [hk, "vecs", rsk], [("hb", kc)])
            if KST == "hb":
                return
            akeys = [("a", c) for c in range(8)]
            if g == 0:
                memset("pool", abuf[:, :, 0:16], 0.0, akeys)
            else:
                T.op("pool", lambda e: e.tensor_copy(out=abuf[:, :, 0:144], in_=abuf[:, :, 512:656]), akeys, akeys)
            if g == NG - 1:
                memset("pool", abuf[:, :, 528:544], 0.0, akeys)
            if g == 0:
                lead = [(0, 512), (512, 640)]
            elif g == NG - 1:
                lead = [(128, 512)]
            else:
                lead = [(128, 640)]
            main = [(0, 512)]
            hbk = [("hb", kc) for kc in range(KC)]
            for n, (typ, ci) in enumerate(CH):
                if n >= int(os.environ.get("KNCH", 1000)):
                    return
                w, wkey = next_w()
                if os.environ.get("KAFTERW"):
                    return
                if typ == "v":
                    blocks = []
                    for (a, b) in lead:
                        blocks += list(range(a // 128, b // 128))
                    for b0 in range(0, len(blocks), 4):
                        bl = blocks[b0:b0 + 4]
                        bank, bkey = next_mm()
                        for bi, blk in enumerate(bl):
                            for kc in range(KC):
                                mm(bank[:, bi * 128:(bi + 1) * 128], hb[:, kc, blk * 128:(blk + 1) * 128], w[:, kc, :],
                                   kc == 0, kc == KC - 1, [("hb", kc), wkey], [bkey],
                                   inc=(kc == KC - 1 and bi == len(bl) - 1))
                        for bi, blk in enumerate(bl):
                            KB = (T0 + blk * 128) // 128
                            ks = KB % 6
                            cp("dve", vbuf[:, ks, ci * 128:(ci + 1) * 128], bank[:, bi * 128:(bi + 1) * 128],
                               [bkey], [("v", ks)])
                    continue
                pieces = lead if typ in ("k", "ag", "av") else main
                for (a, b) in pieces:
                    nt = b - a
                    bank, bkey = next_mm()
                    for kc in range(KC):
                        mm(bank[:, 0:nt], w[:, kc, :], hb[:, kc, a:b], kc == 0, kc == KC - 1,
                           [("hb", kc), wkey], [bkey], inc=(kc == KC - 1))
                    if os.environ.get("KNOEVAC"):
                        continue
                    if typ == "k":
                        for blk in range(a // 128, b // 128):
                            KB = (T0 + blk * 128) // 128
                            ks = KB % 6
                            o0 = blk * 128 - a
                            if True:
                                cp("dve", kbuf[:, ci, ks, :], bank[:, o0:o0 + 128], [bkey], [("k", ci, ks)])
                            else:
                                act(kbuf[:, ci, ks, :], bank[:, o0:o0 + 128], AF.Copy, [bkey], [("k", ci, ks)])
                    elif typ == "ag":
                        act(sgt[ci % 2][:, a:b], bank[:, 0:nt], AF.Sigmoid, [bkey], [("sgt", ci % 2)])
                    elif typ == "av":
                        tt("dve", abuf[:, ci, 16 + a:16 + b], bank[:, 0:nt], sgt[ci % 2][:, a:b], ALU.mult,
                           [bkey, ("sgt", ci % 2)], [("a", ci)])
                    elif typ == "az":
                        act(ybuf[:, ci, :], bank[:, 0:nt], AF.Silu, [bkey], [("y", ci)])
                    elif typ == "bz":
                        act(ybuf[:, 8 + ci, :], bank[:, 0:nt], AF.Silu, [bkey], [("y", 8 + ci)])
                    elif typ == "q":
                        ts("dve", qbuf[:, ci, :], bank[:, 0:nt], 0.125, ALU.mult, [bkey], [("q", ci)])
            if KST == "inproj":
                return
            cwo = VOFF["conv_w"] + l * 8 * 31
            def build_dg(c):
                ds = c % 2
                idb = identb[:]
                cwv = vecs[:, cwo + c * 31:cwo + c * 31 + 31]
                tt(os.environ.get("KDGENG", "dve"), dg[ds][:], bass.AP(idb.tensor, idb.offset, [idb.ap[0], [0, 31], idb.ap[1]]),
                   bass.AP(cwv.tensor, cwv.offset, [cwv.ap[0], cwv.ap[1], [0, 128]]), ALU.mult,
                   ["identb", "vecs"], [("dg", ds)])
            build_dg(0)
            build_dg(1)
            for c in range(8):
                ds = c % 2
                for k in range(31):
                    mm(SC[ds][:, :], dg[ds][:, k, :], abuf[:, c, k + 1:k + 1 + 512], k == 0, k == 30,
                       [("dg", ds), ("a", c)], [("ps", "sc", ds)], inc=(k == 30))
                if c + 2 < 8:
                    build_dg(c + 2)
                act(cyv[:, c, :], SC[ds][:, :], AF.Identity, [("ps", "sc", ds)], [("cy", c)],
                    bias=vcol("conv_b", l * 8 + c))
                tt("dve", ysq[ds][:], cyv[:, c, :], cyv[:, c, :], ALU.mult, [("cy", c)], [("ysq", ds)])
                for cp_ in ([c - 1] if c > 0 else []) + ([7] if c == 7 else []):
                    dp = cp_ % 2
                    mm(ST[0][:, :], onesb[:], cyv[:, cp_, :], cp_ == 0, cp_ == 7, ["onesb", ("cy", cp_)], [("ps", "st", 0)], inc=True)
                    mm(ST[1][:, :], onesb[:], ysq[dp][:], cp_ == 0, cp_ == 7, ["onesb", ("ysq", dp)], [("ps", "st", 1)], inc=True)
            ts("dve", meant[:], ST[0][:, :], 1.0 / 1024, ALU.mult, [("ps", "st", 0)], ["meant"])
            tt("dve", tmp640[:, 0:512], meant[:], meant[:], ALU.mult, ["meant"], ["tmp640"])
            stt("dve", tmp640[:, 0:512], ST[1][:, :], 1.0 / 1024, tmp640[:, 0:512], ALU.mult, ALU.subtract,
                [("ps", "st", 1), "tmp640"], ["tmp640"])
            act(tmp640[:, 0:512], tmp640[:, 0:512], AF.Sqrt, ["tmp640"], ["tmp640"], bias=epsc[:, 0:1], scale=1.0)
            recip(lnA[:], tmp640[:, 0:512], ["tmp640"], ["lnA"])
            stt("dve", lnB[:], meant[:], -1.0, lnA[:], ALU.mult, ALU.mult, ["meant", "lnA"], ["lnB"])
            def ln_apply(c):
                sl = c % 2
                tt("dve", tf[sl][:], cyv[:, c, :], lnA[:], ALU.mult, [("cy", c), "lnA"], [("tf", sl)])
                tt("dve", tf[sl][:], tf[sl][:], lnB[:], ALU.add, [("tf", sl), "lnB"], [("tf", sl)])
                act(tb16[sl][:], tf[sl][:], AF.Silu, [("tf", sl), "vecs"], [("tb16", sl)],
                    bias=vcol("cln_b", l * 8 + c), scale=vcol("cln_g", l * 8 + c))
                tt("pool", ybuf[:, c, :], tb16[sl][:], ybuf[:, c, :], ALU.mult, [("tb16", sl), ("y", c)], [("y", c)])
            ln_todo = list(range(8))
            if not os.environ.get("KLNI"):
                while ln_todo:
                    ln_apply(ln_todo.pop(0))
            if KST == "conv":
                return
            it = 0
            for qb in range(4):
                Q = 4 * g + qb
                kbs = [kb for kb in range(3) if 0 <= Q + kb - 1 < S // 128]
                for j in range(4):
                    pts = []
                    for kb in kbs:
                        ks = (Q + kb - 1) % 6
                        scb = it % 2
                        pti = it % 6
                        it += 1
                        sck = ("ps", "sc", scb)
                        for hh in range(4):
                            c = 2 * j + hh // 2
                            hf = hh % 2
                            mm(SC[scb][:, hh * 128:(hh + 1) * 128], identb[:],
                               biasT[:, 4 * j + hh, kb * 128:(kb + 1) * 128],
                               True, False, ["identb", "biasT"], [sck], inc=False)
                            mm(SC[scb][:, hh * 128:(hh + 1) * 128], kbuf[hf * 64:(hf + 1) * 64, j, ks, :],
                               qbuf[hf * 64:(hf + 1) * 64, c, qb * 128:(qb + 1) * 128], False, True,
                               [("k", j, ks), ("q", c)], [sck], inc=(hh == 3))
                        act(PT[pti][:], SC[scb][:, :], AF.Exp, [sck], [("pt", pti)])
                        pts.append((pti, ks))
                    pvk = ("ps", "pv")
                    if os.environ.get("KATT") == "s":
                        continue
                    for cc in range(2):
                        for hf in range(2):
                            hh = 2 * cc + hf
                            for i, (pti, ks) in enumerate(pts):
                                mm(PV[hf * 64:(hf + 1) * 64, cc * 128:(cc + 1) * 128], vbuf[:, ks, j * 64:(j + 1) * 64],
                                   PT[pti][:, hh * 128:(hh + 1) * 128], i == 0, i == len(pts) - 1,
                                   [("v", ks), ("pt", pti)], [pvk], inc=False, tp=((0, 64) if hf else None))
                            for i, (pti, ks) in enumerate(pts):
                                mm(PV[hf * 64:(hf + 1) * 64, 256 + cc * 128:256 + (cc + 1) * 128], onesb[:, 0:64],
                                   PT[pti][:, hh * 128:(hh + 1) * 128], i == 0, i == len(pts) - 1,
                                   ["onesb", ("pt", pti)], [pvk],
                                   inc=(cc == 1 and hf == 1 and i == len(pts) - 1), tp=((0, 64) if hf else None))
                    dsl = (qb * 4 + j) % 2
                    for cc in range(2):
                        ts("dve", dens[dsl][:, cc * 128:(cc + 1) * 128], PV[:, 256 + cc * 128:256 + (cc + 1) * 128],
                           esink[:, l * 8 + j * 2 + cc:l * 8 + j * 2 + cc + 1], ALU.add, [pvk, "esink"], [("dens", dsl)])
                    recip(dens[dsl][:], dens[dsl][:], [("dens", dsl)], [("dens", dsl)])
                    yv = ybuf[:, 8 + 2 * j:8 + 2 * j + 2, qb * 128:(qb + 1) * 128]
                    yk = [("y", 8 + 2 * j), ("y", 8 + 2 * j + 1)]
                    tt("dve", rz[dsl][:].rearrange("p (c q) -> p c q", c=2), dens[dsl][:].rearrange("p (c q) -> p c q", c=2),
                       yv, ALU.mult, [("dens", dsl)] + yk, [("rz", dsl)])
                    tt("dve", yv, PV[:, 0:256].rearrange("p (c q) -> p c q", c=2),
                       rz[dsl][:].rearrange("p (c q) -> p c q", c=2), ALU.mult, [pvk, ("rz", dsl)], yk)
                    if ln_todo:
                        ln_apply(ln_todo.pop(0))
            while ln_todo:
                ln_apply(ln_todo.pop(0))
            if KST == "attn":
                return
            LA = 2
            pre = {}

            def pf(dc):
                ht, hk, hsn = next_hs()
                dma("sp", hsn, ht[:, 0:G], hT_d[s, dc * 128:(dc + 1) * 128, T0:T0 + G], [("hT", s, dc)], [hk])
                pre[dc] = (ht, hk, hsn)
            for dc in range(LA):
                pf(dc)
            spre = {}
            if nxt is not None:
                T0n = nxt[2] * G
                Wnn = min(640, S - T0n)
                for dc in range(LA):
                    spre[dc] = stats_load(nxt[0], T0n, Wnn, dc)
            for dc in range(16):
                w, wkey = next_w()
                bank, bkey = next_mm()
                for kc in range(KC):
                    mm(bank[:, :], w[:, kc, :], ybuf[:, kc, :], kc == 0, kc == KC - 1, [("y", kc), wkey], [bkey],
                       inc=(kc == KC - 1))
                if dc + LA < 16:
                    pf(dc + LA)
                if nxt is not None:
                    if dc + LA < 16:
                        spre[dc + LA] = stats_load(nxt[0], T0n, Wnn, dc + LA)
                    stats_compute(Wnn, dc, spre.pop(dc))
                ht, hk, hsn = pre.pop(dc)
                tt("dve", ht[:, 0:G], ht[:, 0:G], bank[:, :], ALU.add, [hk, bkey], [hk])
                act(hb[:, dc, 0:G], ht[:, 0:G], AF.Copy, [hk], [("hb", dc)])
                dma("sp", hsn, hT_d[s, dc * 128:(dc + 1) * 128, T0:T0 + G], ht[:, 0:G], [hk], [("hT", s, dc)])
            if nxt is not None:
                stats_finish(Wnn, rsN, "rsN")
                pending_stats[0] = nxt
            if KST == "outproj":
                return
            dma("sp", "ptile", ptile[:], p_d[l, s, T0:T0 + G, :].rearrange("(b p) f -> p b f", p=128), (), ["ptile"])
            for pc in range(2):
                for tb in range(4):
                    tr(ST[pc][:, tb * 128:(tb + 1) * 128], ptile[:, tb, pc * 128:(pc + 1) * 128], ["ptile"],
                       [("ps", "st", pc)], inc=(tb == 3))
                cp("dve", pT[:, pc, :], ST[pc][:, :], [("ps", "st", pc)], [("pT", pc)])
            for dc in range(16):
                bank, bkey = next_mm()
                for pc in range(2):
                    mm(bank[:, :], wpe[:, pc, dc * 128:(dc + 1) * 128], pT[:, pc, :], pc == 0, pc == 1,
                       ["wpe", ("pT", pc)], [bkey], inc=(pc == 1))
                sq4 = [(ysq[0], ("ysq", 0)), (ysq[1], ("ysq", 1)), (tb16[0], ("tb16", 0)), (tb16[1], ("tb16", 1))]
                sqt, sqk = sq4[dc % 4]
                act(ebv[:, dc, :], bank[:, :], AF.Copy, [bkey], [("e", dc)])
                tt("dve", sqt[:], ebv[:, dc, :], ebv[:, dc, :], ALU.mult, [("e", dc)], [sqk])
                for dp in ([dc - 3] if dc >= 3 else []) + ([13, 14, 15] if dc == 15 else []):
                    sqt2, sqk2 = sq4[dp % 4]
                    mm(ST[0][:, :], onesb[:], sqt2[:], dp == 0, dp == 15, ["onesb", sqk2], [("ps", "st", 0)], inc=True)
            act(tmp640[:, 0:512], ST[0][:, :], AF.Sqrt, [("ps", "st", 0)], ["tmp640"], bias=epsc[:, 0:1], scale=1.0 / D)
            recip(rstd[:, 0:512], tmp640[:, 0:512], ["tmp640"], ["rstd"])
            pgo = VOFF["pe_g"] + l * 16
            pre2 = {}

            def pf2(dc):
                ht, hk, hsn = next_hs()
                dma("sp", hsn, ht[:, 0:G], hT_d[s, dc * 128:(dc + 1) * 128, T0:T0 + G], [("hT", s, dc)], [hk])
                pre2[dc] = (ht, hk, hsn)
            for dc in range(LA):
                pf2(dc)
            for dc in range(16):
                w, wkey = next_w()
                bank, bkey = next_mm()
                for kc in range(KC):
                    mm(bank[:, :], w[:, kc, :], hb[:, kc, 0:G], kc == 0, kc == KC - 1, [("hb", kc), wkey], [bkey],
                       inc=(kc == KC - 1))
                if dc + LA < 16:
                    pf2(dc + LA)
                sl = dc % 2
                act(sgb[sl][:], bank[:, :], AF.Sigmoid, [bkey], [("sg", sl)])
                tt("dve", tf[sl][:], ebv[:, dc, :], rstd[:, 0:512], ALU.mult, [("e", dc), "rstd"], [("tf", sl)])
                stt("dve", tf[sl][:], tf[sl][:], vecs[:, pgo + dc:pgo + dc + 1], sgb[sl][:], ALU.mult, ALU.mult,
                    [("tf", sl), "vecs", ("sg", sl)], [("tf", sl)])
                ht, hk, hsn = pre2.pop(dc)
                tt("dve", ht[:, 0:G], ht[:, 0:G], tf[sl][:], ALU.add, [hk, ("tf", sl)], [hk])
                dma("sp", hsn, hT_d[s, dc * 128:(dc + 1) * 128, T0:T0 + G], ht[:, 0:G], [hk], [("hT", s, dc)])

        KSTOP = os.environ.get("KSTOP", "")
        convert(layers[0], 0, NT) if len(layers) else None
        for s in range(nseq):
            if KSTOP == "setup":
                break
            if do_stage0:
                stage0_real(s)
                T.barrier()
            if KSTOP == "stage0":
                break
            for l in layers:
                for pc in range(2):
                    if os.environ.get("KNOWPE"):
                        continue
                    dma("pool", f"wpe{pc}", wpe[:, pc, :], wpe_d[l, :, pc, :], (), ["wpe"], mld=int(os.environ.get("KMLD", 512)))
                for g in range(int(os.environ.get("KGROUPS", NG))):
                    li = layers.index(l)
                    if s == 0 and li + 1 < len(layers) and g == 0:
                        while cv_pending:
                            l2, i2 = cv_pending.pop(0)
                            convert(l2, i2, i2 + 1)
                        cv_pending.extend((layers[li + 1], i) for i in range(NT))
                    ng = int(os.environ.get("KGROUPS", NG))
                    if g + 1 < ng:
                        nxt = (s, l, g + 1)
                    elif li + 1 < len(layers) and ng == NG:
                        nxt = (s, layers[li + 1], 0)
                    else:
                        nxt = None
                    layer_group(s, l, g, nxt)
            while cv_pending:
                l2, i2 = cv_pending.pop(0)
                convert(l2, i2, i2 + 1)
            T.barrier()
            if do_final:
                epilogue(s)
                T.barrier()
        T.barrier()

        names = T.sem_names()
        sems = {n: es.enter_context(nc.semaphore("s_" + n)) for n in names}
        with nc.Block() as block:
            def emit(kind):
                def run(e):
                    for waits, fn, inc in T.ops[kind]:
                        for sname, v in waits:
                            e.wait_ge(sems[sname], v)
                        ins = fn(e)
                        if inc is not None:
                            ins.then_inc(sems[inc[0]], inc[1])
                    for sname, v in T.pending[kind].items():
                        e.wait_ge(sems[sname], v)
                return run
            block.tensor(emit("pe"))
            block.scalar(emit("act"))
            block.vector(emit("dve"))
            block.gpsimd(emit("pool"))
            block.sync(emit("sp"))
    return nc


def prep_shared(norm_g, w_in, conv_w, conv_b, cln_g, cln_b, sink, rel_bias, w_out, w_pe, pe_g, w_pg, final_g):
    f = lambda a: np.ascontiguousarray(np.asarray(a, dtype=np.float32))
    w_in, w_out, w_pe, w_pg = f(w_in), f(w_out), f(w_pe), f(w_pg)
    idx = _chunk_cols()
    wi = w_in[:, :, idx.reshape(-1)].reshape(NL, KC, 128, NCH, 128)
    def padk(a):
        o = np.zeros(a.shape[:3] + (KC + 1, 128), np.float32)
        o[:, :, :, :KC, :] = a
        return o
    wi = padk(wi.transpose(0, 3, 2, 1, 4))
    wo = padk(w_out.reshape(NL, KC, 128, 16, 128).transpose(0, 3, 2, 1, 4))
    wg = padk(w_pg.reshape(NL, KC, 128, 16, 128).transpose(0, 3, 2, 1, 4))
    wp = np.ascontiguousarray(w_pe.reshape(NL, 2, 128, D).transpose(0, 2, 1, 3))
    vecs = _pack_vecs(f(norm_g), f(pe_g), f(final_g), f(conv_b), f(cln_g), f(cln_b), f(conv_w), f(sink))
    buckets, band = _t5_band_buckets()
    gathered = f(rel_bias)[buckets]
    biasg = np.ascontiguousarray(gathered.reshape(128, 3, 128, 16).transpose(3, 2, 1, 0)).reshape(16, 128, 384)
    m01 = band.astype(np.float32).reshape(128, 3, 128).transpose(2, 1, 0).reshape(128, 384)
    masks = np.ascontiguousarray(np.stack([m01, (m01 - 1.0) * 30000.0]).astype(np.float32))
    return {"w_in": wi, "w_out": wo, "w_pg": wg, "w_pe": wp, "vecs": vecs, "biasg": biasg, "masks": masks,
            "ident": np.eye(128, dtype=np.float32)}


def kernel(x, p, norm_g, w_in, conv_w, conv_b, cln_g, cln_b, sink, rel_bias, w_out, w_pe, pe_g, w_pg, final_g):
    x = np.asarray(x, dtype=np.float32)
    p = np.asarray(p, dtype=np.float32)
    shared = prep_shared(norm_g, w_in, conv_w, conv_b, cln_g, cln_b, sink, rel_bias, w_out, w_pe, pe_g, w_pg, final_g)
    nc = build_program()
    in_maps = []
    for c in range(NCORES):
        m = dict(shared)
        m["x"] = np.ascontiguousarray(x[c * SEQ_PER_CORE:(c + 1) * SEQ_PER_CORE])
        m["p"] = np.ascontiguousarray(p[:, c * SEQ_PER_CORE:(c + 1) * SEQ_PER_CORE])
        in_maps.append(m)
    res = run_bass_kernel_spmd(nc, in_maps, core_ids=list(range(NCORES)))
    out = np.concatenate([r["out"] for r in res.results], axis=0)
    return out.astype(np.float32)
```
